# Optimizing a Trainium2 kernel written in Bass

```python
import jax, jax.numpy as jnp
from jax import lax
import numpy as np

D_MODEL = 4096
BATCH = 4
SEQ = 4096
DEPTH = 1

MLA_HEADS = 16
MLA_Q_RANK = 1024
MLA_KV_RANK = 512
MLA_NOPE = 128
MLA_ROPE = 64
MLA_V = 128
MLA_QK = MLA_NOPE + MLA_ROPE
ROPE_THETA = 10000.0
SWA_Q_HEADS = 32
SWA_KV_HEADS = 8
SWA_HEAD_DIM = 64
SWA_GROUP = SWA_Q_HEADS // SWA_KV_HEADS
SWA_WINDOW = 128
BLOCK = 128
MIX_A = MLA_HEADS * MLA_V
MIX_B = SWA_Q_HEADS * SWA_HEAD_DIM
MIX_WIDTH = MIX_A + MIX_B
D_FF = -(-8 * D_MODEL // (3 * 256)) * 256
EPS = 1e-6
IN_SPLITS = (MLA_Q_RANK, MLA_KV_RANK, MLA_ROPE, MIX_B, SWA_KV_HEADS * SWA_HEAD_DIM, SWA_KV_HEADS * SWA_HEAD_DIM)
IN_WIDTH = sum(IN_SPLITS)
SPLIT_IDX = tuple(int(v) for v in np.cumsum(IN_SPLITS)[:-1])

kernel_name = "hybrid_mla_swa_sink_alibi_swiglu_sandwich"


def rms_norm(x, g):
    xf = x.astype(jnp.float32)
    y = xf * lax.rsqrt(jnp.mean(xf * xf, axis=-1, keepdims=True) + EPS)
    return (y * g.astype(jnp.float32)).astype(x.dtype)


def rope_tables(positions):
    inv = 1.0 / (ROPE_THETA ** (jnp.arange(0, MLA_ROPE, 2, dtype=jnp.float32) / MLA_ROPE))
    ang = positions.astype(jnp.float32)[..., None] * inv
    return jnp.cos(ang), jnp.sin(ang)


def apply_rope(x, cos, sin):
    xf = x.astype(jnp.float32)
    x1, x2 = jnp.split(xf, 2, axis=-1)
    return jnp.concatenate([x1 * cos - x2 * sin, x2 * cos + x1 * sin], axis=-1).astype(x.dtype)


def mla_attention(c_q, c_kv, k_rope, cos, sin, q_norm_g, w_uq, kv_norm_g, w_ukv):
    B, S, _ = c_q.shape
    q = (rms_norm(c_q, q_norm_g) @ w_uq).reshape(B, S, MLA_HEADS, MLA_QK)
    q_nope, q_pe = q[..., :MLA_NOPE], q[..., MLA_NOPE:]
    q_pe = apply_rope(q_pe, cos[:, :, None, :], sin[:, :, None, :])
    q = jnp.concatenate([q_nope, q_pe], axis=-1)
    kv = (rms_norm(c_kv, kv_norm_g) @ w_ukv).reshape(B, S, MLA_HEADS, MLA_NOPE + MLA_V)
    k_nope, v = kv[..., :MLA_NOPE], kv[..., MLA_NOPE:]
    k_pe = apply_rope(k_rope, cos, sin)[:, :, None, :]
    k = jnp.concatenate([k_nope, jnp.broadcast_to(k_pe, (B, S, MLA_HEADS, MLA_ROPE))], axis=-1)
    scale = MLA_QK ** -0.5
    nb = S // BLOCK
    qb = q.reshape(B, nb, BLOCK, MLA_HEADS, MLA_QK).transpose(1, 0, 2, 3, 4)
    k_idx = jnp.arange(S)

    def one_block(args):
        qi, i = args
        s = jnp.einsum('bqhd,bkhd->bhqk', qi, k, preferred_element_type=jnp.float32) * scale
        q_idx = i * BLOCK + jnp.arange(BLOCK)
        causal = k_idx[None, :] <= q_idx[:, None]
        s = jnp.where(causal[None, None], s, jnp.finfo(jnp.float32).min)
        p = jax.nn.softmax(s, axis=-1).astype(v.dtype)
        return jnp.einsum('bhqk,bkhd->bqhd', p, v)

    o = lax.map(one_block, (qb, jnp.arange(nb)))
    return o.transpose(1, 0, 2, 3, 4).reshape(B, S, MIX_A)


def swa_attention(q, k, v, positions, sinks):
    B, S, _ = q.shape
    nb = S // BLOCK
    q = q.reshape(B, nb, BLOCK, SWA_KV_HEADS, SWA_GROUP, SWA_HEAD_DIM)
    k = k.reshape(B, S, SWA_KV_HEADS, SWA_HEAD_DIM)
    v = v.reshape(B, S, SWA_KV_HEADS, SWA_HEAD_DIM)

    def band(t):
        pad = [(0, 0), (BLOCK, 0)] + [(0, 0)] * (t.ndim - 2)
        tb = jnp.pad(t, pad).reshape((B, nb + 1, BLOCK) + t.shape[2:])
        return jnp.concatenate([tb[:, :-1], tb[:, 1:]], axis=2)

    k_band, v_band = band(k), band(v)
    k_pos = band(positions)
    q_pos = positions.reshape(B, nb, BLOCK)
    q_idx = jnp.arange(S).reshape(nb, BLOCK)
    k_idx = jnp.arange(-BLOCK, S).reshape(nb + 1, BLOCK)
    k_idx = jnp.concatenate([k_idx[:-1], k_idx[1:]], axis=1)
    delta = q_idx[:, :, None] - k_idx[:, None, :]
    valid = (delta >= 0) & (delta < SWA_WINDOW) & (k_idx[:, None, :] >= 0)
    dist = jnp.abs(q_pos[..., :, None] - k_pos[..., None, :]).astype(jnp.float32)
    slopes = jnp.exp2(-8.0 * jnp.arange(1, SWA_Q_HEADS + 1, dtype=jnp.float32) / SWA_Q_HEADS)
    slopes = slopes.reshape(SWA_KV_HEADS, SWA_GROUP)
    scale = SWA_HEAD_DIM ** -0.5
    s = jnp.einsum('bnqkgd,bnskd->bnkgqs', q, k_band, preferred_element_type=jnp.float32) * scale
    s = s - slopes[None, None, :, :, None, None] * dist[:, :, None, None]
    s = jnp.where(valid[None, :, None, None], s, jnp.finfo(jnp.float32).min)
    sink = sinks.astype(jnp.float32).reshape(SWA_KV_HEADS, SWA_GROUP)[None, None, :, :, None, None]
    m = jnp.maximum(jnp.max(s, axis=-1, keepdims=True), sink)
    e = jnp.exp(s - m)
    p = e / (jnp.sum(e, axis=-1, keepdims=True) + jnp.exp(sink - m))
    o = jnp.einsum('bnkgqs,bnskd->bnqkgd', p.astype(v.dtype), v_band)
    return o.reshape(B, S, MIX_B)


def setup_inputs(seed: int = 0) -> dict:
    key = jax.random.key(seed)
    ks = jax.random.split(key, 20)
    f32 = jnp.float32

    def w(k, shape, fan_in):
        return jax.random.normal(k, shape, f32) * (fan_in ** -0.5)

    def gain(k, n):
        return 1.0 + 0.02 * jax.random.normal(k, (DEPTH, n), f32)

    return {
        "x": jax.random.normal(ks[0], (BATCH, SEQ, D_MODEL), f32),
        "positions": jnp.broadcast_to(jnp.arange(SEQ, dtype=jnp.int32), (BATCH, SEQ)),
        "attn_pre_g": gain(ks[1], D_MODEL),
        "w_in": w(ks[2], (DEPTH, D_MODEL, IN_WIDTH), D_MODEL),
        "q_norm_g": gain(ks[3], MLA_Q_RANK),
        "w_uq": w(ks[4], (DEPTH, MLA_Q_RANK, MLA_HEADS * MLA_QK), MLA_Q_RANK),
        "kv_norm_g": gain(ks[5], MLA_KV_RANK),
        "w_ukv": w(ks[6], (DEPTH, MLA_KV_RANK, MLA_HEADS * (MLA_NOPE + MLA_V)), MLA_KV_RANK),
        "swa_sinks": jax.random.normal(ks[7], (DEPTH, SWA_Q_HEADS), f32),
        "grp_a_g": gain(ks[8], MIX_A),
        "grp_b_g": gain(ks[9], MIX_B),
        "w_o": w(ks[10], (DEPTH, MIX_WIDTH, D_MODEL), MIX_WIDTH),
        "attn_post_g": gain(ks[11], D_MODEL),
        "ffn_pre_g": gain(ks[12], D_MODEL),
        "w_gate": w(ks[13], (DEPTH, D_MODEL, D_FF), D_MODEL),
        "w_up": w(ks[14], (DEPTH, D_MODEL, D_FF), D_MODEL),
        "w_down": w(ks[15], (DEPTH, D_FF, D_MODEL), D_FF),
        "ffn_post_g": gain(ks[16], D_MODEL),
    }


def reference(x, positions, attn_pre_g, w_in, q_norm_g, w_uq, kv_norm_g, w_ukv, swa_sinks,
              grp_a_g, grp_b_g, w_o, attn_post_g, ffn_pre_g, w_gate, w_up, w_down, ffn_post_g):
    cos, sin = rope_tables(positions)
    h = x
    for l in range(DEPTH):
        a = rms_norm(h, attn_pre_g[l])
        proj = a @ w_in[l]
        c_q, c_kv, k_rope, q_s, k_s, v_s = jnp.split(proj, SPLIT_IDX, axis=-1)
        o_a = mla_attention(c_q, c_kv, k_rope, cos, sin, q_norm_g[l], w_uq[l], kv_norm_g[l], w_ukv[l])
        o_b = swa_attention(q_s, k_s, v_s, positions, swa_sinks[l])
        mix = jnp.concatenate([rms_norm(o_a, grp_a_g[l]), rms_norm(o_b, grp_b_g[l])], axis=-1)
        h = h + rms_norm(mix @ w_o[l], attn_post_g[l])
        f = rms_norm(h, ffn_pre_g[l])
        f = (jax.nn.silu(f @ w_gate[l]) * (f @ w_up[l])) @ w_down[l]
        h = h + rms_norm(f, ffn_post_g[l])
    return h
```

```python
import math
from contextlib import ExitStack
import numpy as np
import concourse.bass as bass
import concourse.mybir as mybir
from concourse.bass_utils import run_bass_kernel_spmd

F32 = mybir.dt.float32
BF16 = mybir.dt.bfloat16
I32 = mybir.dt.int32
AF = mybir.ActivationFunctionType
ALU = mybir.AluOpType
AX = mybir.AxisListType

NEG = -30000.0
EPS = 1e-6
SAME_ENGINE_WAITS = True

FULL = dict(D=4096, B=4, SEQ=4096, H=16, QR=1024, KVR=512, SQ=32, SKV=8, DFF=11008, T=512)


def derive(P):
    P = dict(P)
    P["NO"] = P["SEQ"] // 2
    P["KD"] = P["D"] // 128
    P["KQ"] = P["QR"] // 128
    P["KC"] = P["KVR"] // 128
    P["KF"] = P["DFF"] // 128
    P["MIXA"] = P["H"] * 128
    P["MIXB"] = P["SQ"] * 64
    P["KM"] = (P["MIXA"] + P["MIXB"]) // 128
    P["NTILE"] = P["NO"] // P["T"]
    P["CIN"] = P["QR"] + P["KVR"] + P["SQ"] * 64 + P["SKV"] * 128 + P["SKV"] * 64 + 128
    P["NCORE"] = 2 * P["B"]
    blks = []
    def rng(c0, n):
        for i in range(0, n, 256):
            blks.append((c0 + i, min(256, n - i)))
    c = 0
    for n in (P["QR"], P["KVR"], P["SQ"] * 64, P["SKV"] * 128, P["SKV"] * 64, 128):
        rng(c, n)
        c += n
    P["_in_blocks"] = blks
    return P


class Eng:
    def __init__(self, name, sem):
        self.name, self.sem = name, sem
        self.q = []
        self.cnt = 0
        self.seen = {}

    def wait(self, sem, val):
        if val <= 0:
            return
        if sem is self.sem and (not SAME_ENGINE_WAITS or self.name in ("pe", "sp", "pool")):
            return
        if self.seen.get(id(sem), 0) < val:
            self.q.append(("wait", sem, val))
            self.seen[id(sem)] = val


class Buf:
    def __init__(self, name):
        self.name = name
        self.w = {}
        self.r = {}
        self.dsem = None
        self.dcnt = 0
        self.excl = name.startswith("ps") and name[2:].isdigit()


class StopBuild(Exception):
    pass


class Ker:
    def __init__(self, nc, stack):
        self.nc = nc
        self.stack = stack
        self.engs = {}
        for n in ("pe", "act", "dve", "pool", "sp"):
            self.engs[n] = Eng(n, stack.enter_context(nc.semaphore("s_" + n)))
        self.store_tickets = []

    def newsem(self, name):
        return self.stack.enter_context(self.nc.semaphore(name))

    def _deps(self, e, reads, writes):
        need = {}
        def add(t):
            sem, val = t
            k = id(sem)
            if k not in need or need[k][1] < val:
                need[k] = (sem, val)
        for b in reads:
            for t in b.w.values():
                add(t)
            if b.excl:
                for t in b.r.values():
                    add(t)
        for b in writes:
            for t in b.w.values():
                add(t)
            for t in b.r.values():
                add(t)
        for sem, val in need.values():
            e.wait(sem, val)

    def op(self, en, meth, kw, reads=(), writes=()):
        fn = (meth, kw)
        e = self.engs[en]
        self._deps(e, reads, writes)
        e.cnt += 1
        t = (e.sem, e.cnt)
        for b in reads:
            b.r[en] = t
        for b in writes:
            b.w[en] = t
            b.r = {}
        e.q.append(("op", fn, e.sem, 1))

    def dma(self, en, meth, kw, reads=(), writes=(), sembuf=None):
        fn = (meth, kw)
        e = self.engs[en]
        self._deps(e, reads, writes)
        sb = sembuf if sembuf is not None else (writes[0] if writes else reads[0])
        if sb.dsem is None:
            sb.dsem = self.newsem("d_" + sb.name)
        sb.dcnt += 1
        t = (sb.dsem, 16 * sb.dcnt)
        for b in reads:
            b.r["dma_" + sb.name] = t
        for b in writes:
            b.w["dma_" + sb.name] = t
            b.r = {}
        e.q.append(("op", fn, sb.dsem, 16))
        return t

    def barrier(self, names=("pe", "act", "dve", "sp")):
        snap = [(self.engs[n].sem, self.engs[n].cnt) for n in names]
        for n in names:
            e = self.engs[n]
            for sem, val in snap:
                if sem is not e.sem:
                    e.wait(sem, val)
            for sem, val in self.store_tickets:
                e.wait(sem, val)

    def emit(self, block):
        nc = self.nc
        hmap = {"pe": "tensor", "act": "scalar", "dve": "vector", "pool": "gpsimd", "sp": "sync"}
        for n, e in self.engs.items():
            def body(h, e=e):
                for it in e.q:
                    if it[0] == "wait":
                        h.wait_ge(it[1], it[2])
                    else:
                        getattr(h, it[1][0])(**it[1][1]).then_inc(it[2], it[3])
            getattr(block, hmap[n])(body)


def build(P):
    D, NO, T, KD, KQ, KC, KF, KM = P["D"], P["NO"], P["T"], P["KD"], P["KQ"], P["KC"], P["KF"], P["KM"]
    H, SQ, SKV, QR, KVR, DFF, CIN = P["H"], P["SQ"], P["SKV"], P["QR"], P["KVR"], P["DFF"], P["CIN"]
    NTILE = P["NTILE"]
    NB = T // 128
    NK = 2 * NO
    KMAX = max(KD, KM)
    assert T == 512 and KF * 128 == DFF

    nc = bass.Bass("TRN2", target_bir_lowering=False)
    dt_in = lambda n, s, d=F32: nc.dram_tensor(n, s, d, kind="ExternalInput").ap()
    x_own = dt_in("x_own", [NO, D])
    x_ctx = dt_in("x_ctx", [NO, D])
    pos_own = dt_in("pos_own", [1, NO], I32)
    pos_ctx = dt_in("pos_ctx", [1, NO], I32)
    w_in = dt_in("w_in", [128, KD * CIN])
    w_uq = dt_in("w_uq", [128, H * KQ * 256])
    w_ukv = dt_in("w_ukv", [128, H * KC * 256])
    w_o = dt_in("w_o", [128, (D // 256) * KM * 256])
    w_gate = dt_in("w_gate", [128, KF * KD * 128])
    w_up = dt_in("w_up", [128, KF * KD * 128])
    w_down = dt_in("w_down", [128, (D // 512) * KF * 512])
    g_rows = dt_in("g_rows", [4, D])
    NCOL = KQ + KC + KM
    g_cols = dt_in("g_cols", [128, NCOL])
    sinks = dt_in("sinks", [1, SQ])
    NCST = 128 + 896 + 128 + 8
    cst = dt_in("cst", [128, NCST])
    out = nc.dram_tensor("out", [NO, D], F32, kind="ExternalOutput").ap()

    stack = ExitStack()
    with stack:
        K = Ker(nc, stack)
        ARENA = 192 * 1024
        arena = stack.enter_context(nc.sbuf_tensor("arena", [128, ARENA // 2], BF16))
        psum = [stack.enter_context(nc.psum_tensor("ps%d" % i, [128, 512], F32)) for i in range(8)]
        PS = [Buf("ps%d" % i) for i in range(8)]

        def region(off, nbytes, dtype, shape=None):
            assert off % 4 == 0 and off + nbytes <= ARENA, (off, nbytes)
            a = arena[:, off // 2:(off + nbytes) // 2]
            if dtype is F32 or dtype is I32:
                a = a.bitcast(dtype)
            return a

        KB = 1024
        o_cst = 0
        o_ckv = 5 * KB
        sz_ckv = KC * NK * 2
        o_kpe = o_ckv + sz_ckv
        o_slots = o_kpe + NK * 2
        SLOT = KMAX * 256 * 2
        NSLOT = 3
        o_carry = o_slots + NSLOT * SLOT
        o_small = o_carry + SKV * 128 * 2 + SKV * 64 * 2
        o_small = (o_small + 3) // 4 * 4
        o_cs = o_small + 2 * KB
        o_stage = o_cs + 2 * T * 4
        STAGE = ARENA - o_stage
        print("arena: stage offset", o_stage, "stage bytes", STAGE)

        cst_t = region(o_cst, NCST * 4, F32)
        ident_f = cst_t[:, 0:128]
        cmask = cst_t[:, 128:128 + 896]
        pmask = cst_t[:, 1024:1152]
        ccol = cst_t[:, 1152:1160]
        B_cst = Buf("cst")
        small = region(o_small, 2 * KB, F32)
        B_small = Buf("small")
        gcol = small[:, 0:NCOL]
        esink = small[:, 64:64 + SQ]
        small_next = [64 + SQ]
        def cell(n=1):
            a = small[:, small_next[0]:small_next[0] + n]
            small_next[0] += n
            assert small_next[0] <= 512 - 128
            return a
        ss_cells, rs_cells, ssp_cells = cell(2 * NB), cell(2 * NB), cell(NB * (D // 256))
        G_xs = [Buf("gxs0"), Buf("gxs1")]
        G_g, G_pos, G_tmp = Buf("gg"), Buf("gpos"), Buf("gtmp")
        identb = region(o_small + 2 * KB - 512, 256, BF16)
        onesb = region(o_small + 2 * KB - 256, 256, BF16)
        B_ident = Buf("ident")

        ckvT = region(o_ckv, sz_ckv, BF16).rearrange("p (k n) -> p k n", k=KC)
        B_ckv = [Buf("ckv%d" % i) for i in range(NK // 512)]
        kpeT = region(o_kpe, NK * 2, BF16)
        B_kpe = [Buf("kpe%d" % i) for i in range(NK // 512)]
        slots = [region(o_slots + i * SLOT, SLOT, BF16) for i in range(NSLOT)]
        B_slot = [Buf("slot%d" % i) for i in range(NSLOT)]
        kd_prev = region(o_carry, SKV * 128 * 2, BF16).rearrange("p (h n) -> p h n", h=SKV)
        v_prev = region(o_carry + SKV * 128 * 2, SKV * 64 * 2, BF16)
        B_carry = Buf("carry")

        def sreg(off, nbytes, dtype):
            assert off + nbytes <= STAGE, ("stage overflow", off, nbytes, STAGE)
            return region(o_stage + off, nbytes, dtype)

        wplan = []
        wstate = {"issued": 0, "cur": 0}

        def w_issue_upto(i):
            while wstate["issued"] <= min(i, len(wplan) - 1):
                j = wstate["issued"]
                key, parts = wplan[j]
                s = j % NSLOT
                for (dstf, src) in parts:
                    dst = dstf(slots[s])
                    K.dma("pool", "dma_start", dict(out=dst, in_=src),
                          writes=[B_slot[s]])
                wstate["issued"] += 1

        def w_get(key):
            j = wstate["cur"]
            assert wplan[j][0] == key, (wplan[j][0], key)
            w_issue_upto(j + NSLOT - 1)
            wstate["cur"] += 1
            return slots[j % NSLOT], B_slot[j % NSLOT]

        def slot_view(slot, kk, cols):
            return slot[:, 0:kk * cols].rearrange("p (k c) -> p k c", k=kk)

        def chunked(ap, n):
            b_ = 2048
            while n % b_:
                b_ //= 2
            return ap.rearrange("p (a b) -> p a b", b=b_)

        c_q0 = 0
        c_kv0 = QR
        c_qs0 = QR + KVR
        c_kd0 = c_qs0 + SQ * 64
        c_vs0 = c_kd0 + SKV * 128
        c_kr0 = c_vs0 + SKV * 64
        def inproj_blocks(kind):
            blks = []
            def rng(name, c0, n):
                for i in range(0, n, 256):
                    blks.append((name, i // 128, c0 + i, min(256, n - i)))
            if kind == "own":
                rng("cq", c_q0, QR)
            rng("ckv", c_kv0, KVR)
            if kind in ("own", "ctxlast"):
                if kind == "own":
                    rng("qs", c_qs0, SQ * 64)
                rng("kd", c_kd0, SKV * 128)
                rng("vs", c_vs0, SKV * 64)
            rng("kr", c_kr0, 128)
            return blks

        def plan():
            def flat(dram, off, n, soff=0):
                return (lambda s, n=n, soff=soff: chunked(s[:, soff:soff + n], n), chunked(dram[:, off:off + n], n))
            in_off = {}
            o = 0
            for (name, mc, c0, n) in inproj_blocks("own"):
                in_off[(name, mc)] = o
                o += KD * n
            assert o == KD * CIN
            for t in range(NTILE):
                kind = "ctxlast" if t == NTILE - 1 else "ctx"
                for (name, mc, c0, n) in inproj_blocks(kind):
                    wplan.append((("in", "c", t, name, mc), [flat(w_in, in_off[(name, mc)], KD * n)]))
            for t in range(NTILE):
                for (name, mc, c0, n) in inproj_blocks("own"):
                    wplan.append((("in", "o", t, name, mc), [flat(w_in, in_off[(name, mc)], KD * n)]))
                for h in range(H):
                    wplan.append((("head", t, h), [flat(w_uq, h * KQ * 256, KQ * 256),
                                                   flat(w_ukv, h * KC * 256, KC * 256, KQ * 256)]))
                for cb in range(D // 256):
                    wplan.append((("wo", t, cb), [flat(w_o, cb * KM * 256, KM * 256)]))
            for t in range(NTILE):
                for j in range(KF):
                    wplan.append((("gu", t, j), [flat(w_gate, j * KD * 128, KD * 128),
                                                 flat(w_up, j * KD * 128, KD * 128, KD * 128)]))
                KG = SLOT // 2 // 512
                for cb in range(D // 512):
                    for k0 in range(0, KF, KG):
                        kn = min(KG, KF - k0)
                        wplan.append((("dn", t, cb, k0), [flat(w_down, (cb * KF + k0) * 512, kn * 512)]))
        plan()
        assert KQ * 256 + KC * 256 <= SLOT // 2

        def mm(out_ap, lhsT, rhs, start, stop, reads, writes, tp=None):
            kw = {}
            if tp is not None:
                kw["tile_position"] = tp
            K.op("pe", "matmul", dict(out=out_ap, lhsT=lhsT, rhs=rhs, start=start, stop=stop, **kw), reads=reads, writes=writes)

        evac_rr = [0]
        def copy_any(out_ap, in_ap, reads, writes, eng=None):
            if eng is None:
                eng = ("act", "dve")[evac_rr[0] % 2]
                evac_rr[0] += 1
            if eng == "act":
                K.op("act", "activation", dict(out=out_ap, in_=in_ap, func=AF.Copy), reads=reads, writes=writes)
            else:
                K.op(eng, "tensor_copy", dict(out=out_ap, in_=in_ap), reads=reads, writes=writes)

        def rstd_from(dst, src, n_feat, reads, writes):
            K.op("act", "activation", dict(out=dst, in_=src, func=AF.Sqrt, scale=1.0 / n_feat, bias=ccol[:dst.shape[0], 4:5]),
                 reads=list(reads) + [B_cst], writes=writes)
            K.op("dve", "reciprocal", dict(out=dst, in_=dst), reads=writes, writes=writes)

        K.dma("sp", "dma_start", dict(out=cst_t, in_=cst), writes=[B_cst])
        K.dma("sp", "dma_start", dict(out=gcol, in_=g_cols), writes=[B_small])
        K.dma("sp", "dma_start", dict(out=esink, in_=sinks.partition_broadcast(128)), writes=[B_small])
        K.op("act", "activation", dict(out=esink, in_=esink, func=AF.Exp), reads=[B_small], writes=[B_small])
        K.op("dve", "tensor_copy", dict(out=identb, in_=ident_f), reads=[B_cst], writes=[B_ident])
        K.op("dve", "memset", dict(ap=onesb, constant=1.0), writes=[B_ident])
        K.op("dve", "memset", dict(ap=kpeT[64:128, :], constant=0.0), writes=B_kpe)

        ps_rr = {"n": 0}

        def norm_transpose(src_rows_fn, B_src, grow, aT, B_aT, o_xs, o_g, o_abf):
            xs = [sreg(o_xs + i * D * 4, D * 4, F32) for i in range(2)]
            B_xs = G_xs
            g_bc = sreg(o_g, D * 4, F32)
            B_g = G_g
            abf = [sreg(o_abf + i * D * 2, D * 2, BF16) for i in range(1)]
            B_abf = [Buf("abf0")]
            ss = ss_cells
            B_ss = Buf("ss")
            K.dma("sp", "dma_start", dict(out=g_bc, in_=g_rows[grow:grow + 1, :].partition_broadcast(128)), writes=[B_g])
            for blk in range(NB):
                x_t, B_x = xs[blk % 2], B_xs[blk % 2]
                a_t, B_a = abf[0], B_abf[0]
                src = src_rows_fn(blk)
                K.dma("sp", "dma_start", dict(out=x_t, in_=src), reads=[B_src] if B_src else [], writes=[B_x])
                s1 = ss[:, 2 * blk:2 * blk + 1]
                s2 = ss[:, 2 * blk + 1:2 * blk + 2]
                B_ssb = Buf("ss%d" % blk)
                K.op("act", "activation", dict(out=aT[:, :, blk * 128:(blk + 1) * 128], in_=x_t.rearrange("p (k n) -> p k n", k=KD), func=AF.Square, accum_out=s1),
                     reads=[B_x], writes=[B_ssb])
                rstd_from(s2, s1, D, [B_ssb], [B_ssb])
                K.op("dve", "scalar_tensor_tensor", dict(out=a_t, in0=x_t, scalar=s2, in1=g_bc, op0=ALU.mult, op1=ALU.mult),
                     reads=[B_x, B_ssb, B_g], writes=[B_a])
                for c0 in range(0, KD, 8):
                    cn = min(8, KD - c0)
                    pi = ps_rr["n"] % 2
                    ps_rr["n"] += 1
                    pb = psum[pi][:, :].bitcast(BF16)
                    for c in range(cn):
                        K.op("pe", "transpose", dict(out=pb[:, c * 128:(c + 1) * 128], in_=a_t[:, (c0 + c) * 128:(c0 + c + 1) * 128], identity=identb),
                             reads=[B_a, B_ident], writes=[PS[pi]])
                    copy_any(aT[:, c0:c0 + cn, blk * 128:(blk + 1) * 128], pb[:, 0:cn * 128].rearrange("p (c n) -> p c n", c=cn),
                             reads=[PS[pi]], writes=[B_aT])

        def rope_tables(pos_ap, cosT, sinT, tmp, tmpi, B_tab, B_tmp):
            K.dma("sp", "dma_start", dict(out=tmpi, in_=pos_ap.partition_broadcast(64)), writes=[B_tmp])
            K.op("dve", "tensor_copy", dict(out=tmp, in_=tmpi), reads=[B_tmp], writes=[B_tmp])
            for (dst, shcol) in ((cosT, ccol[0:64, 5:6]), (sinT, ccol[0:64, 1:2])):
                K.op("dve", "tensor_scalar", dict(out=dst, in0=tmp, scalar1=ccol[0:64, 0:1], scalar2=shcol, op0=ALU.mult, op1=ALU.add),
                     reads=[B_tmp, B_cst], writes=[B_tab])
                K.op("dve", "tensor_scalar", dict(out=tmpi, in0=dst, scalar1=1.0 / (2 * math.pi), scalar2=None, op0=ALU.mult),
                     reads=[B_tab], writes=[B_tmp])
                K.op("dve", "tensor_copy", dict(out=tmp2_holder[0], in_=tmpi), reads=[B_tmp], writes=[B_tmp])
                K.op("dve", "scalar_tensor_tensor", dict(out=dst, in0=tmp2_holder[0], scalar=-2 * math.pi, in1=dst, op0=ALU.mult, op1=ALU.add),
                     reads=[B_tmp, B_tab], writes=[B_tab])
                K.op("dve", "tensor_scalar", dict(out=tmp2_holder[0], in0=dst, scalar1=math.pi, scalar2=-2 * math.pi, op0=ALU.is_gt, op1=ALU.mult),
                     reads=[B_tab], writes=[B_tmp])
                K.op("dve", "tensor_tensor", dict(out=dst, in0=dst, in1=tmp2_holder[0], op=ALU.add), reads=[B_tab, B_tmp], writes=[B_tab])
                K.op("dve", "tensor_scalar", dict(out=tmp2_holder[0], in0=dst, scalar1=-math.pi, scalar2=2 * math.pi, op0=ALU.is_lt, op1=ALU.mult),
                     reads=[B_tab], writes=[B_tmp])
                K.op("dve", "tensor_tensor", dict(out=dst, in0=dst, in1=tmp2_holder[0], op=ALU.add), reads=[B_tab, B_tmp], writes=[B_tab])
                K.op("dve", "tensor_scalar", dict(out=dst, in0=dst, scalar1=-3.1415925, scalar2=3.1415925, op0=ALU.max, op1=ALU.min),
                     reads=[B_tab], writes=[B_tab])
                K.op("act", "activation", dict(out=dst, in_=dst, func=AF.Sin), reads=[B_tab], writes=[B_tab])
        tmp2_holder = [None]

        def inproj(kind, t, aT, B_aT, kt, pos_ap, o0, outs):
            sq = sreg(o0, T * 2, BF16); B_sq = Buf("sq")
            rstd = sreg(o0 + 1 * KB, T * 4, F32); B_rstd = Buf("rstd")
            cosT = region(o_cs, T * 4, F32)[0:64, :]
            sinT = region(o_cs + T * 4, T * 4, F32)[0:64, :]
            tmp = sreg(o0 + 3 * KB, T * 4, F32)[0:64, :]
            tmpi = sreg(o0 + 5 * KB, T * 4, I32)[0:64, :]
            tmp2_holder[0] = sreg(o0 + 7 * KB, T * 4, F32)[0:64, :]
            B_tab, B_tmp = Buf("tab"), G_tmp
            rope_tables(pos_ap, cosT, sinT, tmp, tmpi, B_tab, B_tmp)
            stop_if("rope")
            kc0 = kt * T
            stat = {}
            acc_rr = [0]
            def acc_bank():
                i = 2 + acc_rr[0] % 3
                acc_rr[0] += 1
                return i
            nchunks = {"cq": KQ, "ckv": KC}
            done = {"cq": 0, "ckv": 0}
            ck = "c" if kind != "own" else "o"
            for (name, mc, c0, n) in inproj_blocks(kind):
                slot, B_s = w_get(("in", ck, t, name, mc))
                sv = slot_view(slot, KD, n)
                stop_if("blk")
                if STOP == "w1":
                    K.op("dve", "tensor_copy", dict(out=kpeT[:, 0:64], in_=sv[:, 0, 0:64]), reads=[B_s], writes=[B_kpe[0]])
                    stop_if("w1")
                if name in ("cq", "ckv", "qs", "kd"):
                    for m in range(n // 128):
                        pi = acc_bank()
                        for k in range(KD):
                            mm(psum[pi][:, :], sv[:, k, m * 128:(m + 1) * 128], aT[:, k, :], k == 0, k == KD - 1,
                               [B_s, B_aT], [PS[pi]])
                        ch = mc + m
                        if name in ("cq", "ckv"):
                            if name == "cq":
                                dst, B_d, sp_i = outs["cqT"][:, ch, :], outs["B_cq"], 6
                            else:
                                dst, B_d, sp_i = ckvT[:, ch, kc0:kc0 + T], B_ckv[kt], 6
                            K.op("act", "activation", dict(out=dst, in_=psum[pi][:, :], func=AF.Copy), reads=[PS[pi]], writes=[B_d])
                            K.op("dve", "tensor_tensor", dict(out=sq, in0=dst, in1=dst, op=ALU.mult), reads=[B_d], writes=[B_sq])
                            mm(psum[sp_i][:, :], onesb, sq, done[name] == 0, done[name] == nchunks[name] - 1, [B_sq, B_ident], [PS[sp_i]])
                            done[name] += 1
                            if done[name] == nchunks[name]:
                                nf = QR if name == "cq" else KVR
                                goff = 0 if name == "cq" else KQ
                                rstd_from(rstd, psum[sp_i][:, :], nf, [PS[sp_i]], [B_rstd])
                                for c in range(nchunks[name]):
                                    d2 = outs["cqT"][:, c, :] if name == "cq" else ckvT[:, c, kc0:kc0 + T]
                                    K.op("dve", "scalar_tensor_tensor", dict(out=d2, in0=d2, scalar=gcol[:, goff + c:goff + c + 1], in1=rstd, op0=ALU.mult, op1=ALU.mult),
                                         reads=[B_d, B_rstd, B_small], writes=[B_d])
                        elif name == "qs":
                            copy_any(outs["qsT"][:, ch, :], psum[pi][:, :], [PS[pi]], [outs["B_qs"]])
                        else:
                            copy_any(outs["kdT"][:, ch, 128:128 + T], psum[pi][:, :], [PS[pi]], [outs["B_kd"]])
                elif name == "vs":
                    for blk in range(NB):
                        pi = acc_bank()
                        for k in range(KD):
                            mm(psum[pi][:, 0:n], aT[:, k, blk * 128:(blk + 1) * 128], sv[:, k, 0:n], k == 0, k == KD - 1, [B_s, B_aT], [PS[pi]])
                        copy_any(outs["vs"][:, 1 + blk, c0 - c_vs0:c0 - c_vs0 + n], psum[pi][:, 0:n], [PS[pi]], [outs["B_vs"]])
                else:
                    pa, pb_ = acc_bank(), acc_bank()
                    for k in range(KD):
                        mm(psum[pa][0:64, :], sv[:, k, 0:64], aT[:, k, :], k == 0, k == KD - 1, [B_s, B_aT], [PS[pa]])
                    for k in range(KD):
                        mm(psum[pb_][0:64, :], sv[:, k, 64:128], aT[:, k, :], k == 0, k == KD - 1, [B_s, B_aT], [PS[pb_]])
                    K.op("dve", "tensor_tensor", dict(out=tmp, in0=psum[pa][0:64, :], in1=cosT, op=ALU.mult), reads=[PS[pa], B_tab], writes=[B_tmp])
                    K.op("dve", "tensor_tensor", dict(out=tmp2_holder[0], in0=psum[pb_][0:64, :], in1=sinT, op=ALU.mult), reads=[PS[pb_], B_tab], writes=[B_tmp])
                    K.op("dve", "tensor_tensor", dict(out=kpeT[0:64, kc0:kc0 + T], in0=tmp, in1=tmp2_holder[0], op=ALU.add), reads=[B_tmp], writes=[B_kpe[kt]])
            return cosT, sinT, B_tab

        def postnorm(y_bf, B_y, sspart, B_ssp, nparts, grow, res_fn, B_res_list, dst_fn, B_dst, o_xs, o_g):
            xs = [sreg(o_xs + i * D * 4, D * 4, F32) for i in range(2)]
            B_xs = G_xs
            g_bc = sreg(o_g, D * 4, F32); B_g = G_g
            K.dma("sp", "dma_start", dict(out=g_bc, in_=g_rows[grow:grow + 1, :].partition_broadcast(128)), writes=[B_g])
            rs = rs_cells; B_rs = Buf("rs")
            for blk in range(NB):
                K.dma("sp", "dma_start", dict(out=xs[0], in_=res_fn(blk)), reads=B_res_list, writes=[B_xs[0]])
                a, b = rs[:, 2 * blk:2 * blk + 1], rs[:, 2 * blk + 1:2 * blk + 2]
                K.op("dve", "tensor_reduce", dict(out=a, in_=sspart[:, blk * nparts:(blk + 1) * nparts], axis=AX.X, op=ALU.add), reads=[B_ssp], writes=[B_rs])
                rstd_from(b, a, D, [B_rs], [B_rs])
                K.op("dve", "scalar_tensor_tensor", dict(out=xs[1], in0=y_bf[:, blk, :], scalar=b, in1=g_bc, op0=ALU.mult, op1=ALU.mult),
                     reads=[B_y, B_rs, B_g], writes=[B_xs[1]])
                K.op("dve", "tensor_tensor", dict(out=xs[0], in0=xs[0], in1=xs[1], op=ALU.add), reads=[B_xs[0], B_xs[1]], writes=[B_xs[0]])
                tk = K.dma("sp", "dma_start", dict(out=dst_fn(blk), in_=xs[0]), reads=[B_xs[0]], writes=[B_dst])
                K.store_tickets.append(tk)

        o_aT = 0
        o_xs = 32 * KB if KD * T * 2 <= 32 * KB else KD * T * 2
        sz_aT = KD * T * 2
        def stage_A1(src_fn, B_src):
            aT = sreg(o_aT, sz_aT, BF16).rearrange("p (k n) -> p k n", k=KD)
            B_aT = Buf("aT")
            norm_transpose(src_fn, B_src, 0, aT, B_aT, sz_aT, sz_aT + 2 * D * 4, sz_aT + 3 * D * 4)
            return aT, B_aT

        o_io = max(sz_aT, KM * T * 2)
        sz_cq = KQ * T * 2
        sz_qs = (SQ // 2) * T * 2
        sz_kd = SKV * (128 + T) * 2
        sz_vs = (1 + NB) * SKV * 64 * 2
        o_cq, o_qs = o_io, o_io + sz_cq
        o_kd = o_qs + sz_qs
        o_vs = o_kd + sz_kd
        o_tmpA = (o_vs + sz_vs + 3) // 4 * 4

        def io_views():
            outs = {}
            outs["cqT"] = sreg(o_cq, sz_cq, BF16).rearrange("p (k n) -> p k n", k=KQ)
            outs["qsT"] = sreg(o_qs, sz_qs, BF16).rearrange("p (k n) -> p k n", k=SQ // 2)
            outs["kdT"] = sreg(o_kd, sz_kd, BF16).rearrange("p (h n) -> p h n", h=SKV)
            outs["vs"] = sreg(o_vs, sz_vs, BF16).rearrange("p (b c) -> p b c", b=1 + NB)
            for n in ("cq", "qs", "kd", "vs"):
                outs["B_" + n] = Buf(n)
            return outs

        B_out = Buf("out_dram")

        STOP = P.get("stop")
        stopn = [P.get("stopn", 0)]
        def stop_if(tag):
            if STOP == tag:
                if stopn[0] == 0:
                    raise StopBuild()
                stopn[0] -= 1
        def build_phases():
            for t in range(NTILE):
                K.barrier()
                stop_if("const")
                aT, B_aT = stage_A1(lambda blk, t=t: x_ctx[t * T + blk * 128:t * T + (blk + 1) * 128, :], None)
                stop_if("a1")
                K.barrier()
                outs = io_views()
                kind = "ctxlast" if t == NTILE - 1 else "ctx"
                inproj(kind, t, aT, B_aT, t, pos_ctx[:, t * T:(t + 1) * T], o_tmpA, outs)
                if kind == "ctxlast":
                    K.op("dve", "tensor_copy", dict(out=kd_prev, in_=outs["kdT"][:, :, T:T + 128]), reads=[outs["B_kd"]], writes=[B_carry])
                    K.op("dve", "tensor_copy", dict(out=v_prev, in_=outs["vs"][:, NB, :]), reads=[outs["B_vs"]], writes=[B_carry])

            stop_if("ctx")
            sc_mla = 1.0 / math.sqrt(192.0)
            sc_swa = 1.0 / 8.0
            slopes = [2.0 ** (-8.0 * (i + 1) / SQ) for i in range(SQ)]
            for t in range(NTILE):
                K.barrier()
                aT, B_aT = stage_A1(lambda blk, t=t: x_own[t * T + blk * 128:t * T + (blk + 1) * 128, :], None)
                K.barrier()
                outs = io_views()
                kt = NTILE + t
                K.op("dve", "tensor_copy", dict(out=outs["kdT"][:, :, 0:128], in_=kd_prev), reads=[B_carry], writes=[outs["B_kd"]])
                K.op("dve", "tensor_copy", dict(out=outs["vs"][:, 0, :], in_=v_prev), reads=[B_carry], writes=[outs["B_vs"]])
                stop_if("carry")
                cosT, sinT, B_tab = inproj("own", t, aT, B_aT, kt, pos_own[:, t * T:(t + 1) * T], o_tmpA, outs)
                K.op("dve", "tensor_copy", dict(out=kd_prev, in_=outs["kdT"][:, :, T:T + 128]), reads=[outs["B_kd"]], writes=[B_carry])
                K.op("dve", "tensor_copy", dict(out=v_prev, in_=outs["vs"][:, NB, :]), reads=[outs["B_vs"]], writes=[B_carry])
                if t == 0:
                    stop_if("inproj0")
                K.barrier()
                mixT = sreg(o_aT, KM * T * 2, BF16).rearrange("p (k n) -> p k n", k=KM)
                B_mix = [Buf("mix%d" % i) for i in range(KM)]
                o1 = o_tmpA
                posq_i = sreg(o1, 512, I32); posq = sreg(o1 + 512, 512, F32)
                posk_i = sreg(o1 + 1024, 8, I32)[:, 0:2]; posk = sreg(o1 + 1032, 8, F32)[:, 0:2]
                dist = [sreg(o1 + 2 * KB + i * 512, 512, F32) for i in range(2)]
                bias4 = sreg(o1 + 3 * KB, 2 * KB, F32).rearrange("p (g n) -> p g n", g=4)
                stmp = sreg(o1 + 5 * KB, 2 * KB, F32)
                pT = [sreg(o1 + 7 * KB + i * KB, KB, BF16) for i in range(2)]
                rden = sreg(o1 + 9 * KB, KB, F32)
                sqB = sreg(o1 + 10 * KB, 512, BF16)
                B_pos, B_dist, B_bias, B_stmp, B_rden, B_sqB = G_pos, Buf("dist"), Buf("bias4"), Buf("stmp"), Buf("rden"), Buf("sqB")
                B_pT = [Buf("pT0"), Buf("pT1")]
                STB = 7
                cnt_statB = [0] * NB
                for jq in range(NB):
                    q0 = t * T + jq * 128
                    K.dma("sp", "dma_start", dict(out=posq_i, in_=pos_own[:, q0:q0 + 128].partition_broadcast(128)), writes=[B_pos])
                    if q0 == 0:
                        prev_src = pos_ctx[:, NO - 128:NO]
                    else:
                        prev_src = pos_own[:, q0 - 128:q0]
                    K.dma("sp", "dma_start", dict(out=posk_i[:, 0:1], in_=prev_src.rearrange("o n -> n o")), writes=[B_pos])
                    K.dma("sp", "dma_start", dict(out=posk_i[:, 1:2], in_=pos_own[:, q0:q0 + 128].rearrange("o n -> n o")), writes=[B_pos])
                    K.op("dve", "tensor_copy", dict(out=posq, in_=posq_i), reads=[B_pos], writes=[B_pos])
                    K.op("dve", "tensor_copy", dict(out=posk, in_=posk_i), reads=[B_pos], writes=[B_pos])
                    K.op("dve", "tensor_scalar", dict(out=posk, in0=posk, scalar1=-1.0, scalar2=None, op0=ALU.mult), reads=[B_pos], writes=[B_pos])
                    for kb in range(2):
                        K.op("act", "activation", dict(out=dist[kb], in_=posq, func=AF.Abs, bias=posk[:, kb:kb + 1]),
                             reads=[B_pos], writes=[B_dist])
                    stop_if("s")
                    for kvh in range(SKV):
                        o_ps, d_ps = 5, 6
                        for kb in range(2):
                            kcol0 = jq * 128 + kb * 128
                            for par in range(2):
                                s_ps = 2 * kb + par
                                pb = par * 64
                                for cl in range(2):
                                    qh = kvh * 4 + cl * 2 + par
                                    c = qh // 2
                                    mm(psum[s_ps][:, cl * 128:(cl + 1) * 128], outs["kdT"][pb:pb + 64, kvh, kcol0:kcol0 + 128],
                                       outs["qsT"][pb:pb + 64, c, jq * 128:(jq + 1) * 128], True, True, [outs["B_kd"], outs["B_qs"]], [PS[s_ps]],
                                       tp=(pb, 0) if pb else None)
                            stop_if("s")
                            msk = pmask if kb == 0 else cmask[:, 384:512]
                            for par in range(2):
                                for cl in range(2):
                                    qh = kvh * 4 + cl * 2 + par
                                    K.op("dve", "scalar_tensor_tensor", dict(out=bias4[:, par * 2 + cl, :], in0=dist[kb], scalar=-slopes[qh], in1=msk, op0=ALU.mult, op1=ALU.add),
                                         reads=[B_dist, B_cst], writes=[B_bias])
                            for par in range(2):
                                s_ps = 2 * kb + par
                                K.op("dve", "scalar_tensor_tensor", dict(out=stmp[:, par * 256:(par + 1) * 256], in0=psum[s_ps][:, 0:256], scalar=sc_swa,
                                                                         in1=bias4[:, par * 2:par * 2 + 2, :].rearrange("p g n -> p (g n)"), op0=ALU.mult, op1=ALU.add),
                                     reads=[PS[s_ps], B_bias], writes=[B_stmp])
                            stop_if("s")
                            if kb == 0 and q0 == 0:
                                K.op("act", "activation", dict(out=pT[kb], in_=stmp, func=AF.Exp, bias=ccol[:, 2:3]), reads=[B_stmp, B_cst], writes=[B_pT[kb]])
                            else:
                                K.op("act", "activation", dict(out=pT[kb], in_=stmp, func=AF.Exp), reads=[B_stmp], writes=[B_pT[kb]])
                            stop_if("s")
                        for g in range(4):
                            par, cl = g % 2, g // 2
                            for kb in range(2):
                                mm(psum[o_ps][par * 64:(par + 1) * 64, cl * 128:(cl + 1) * 128], outs["vs"][:, jq + kb, kvh * 64:(kvh + 1) * 64],
                                   pT[kb][:, (par * 2 + cl) * 128:(par * 2 + cl + 1) * 128], kb == 0, kb == 1, [outs["B_vs"], B_pT[kb]], [PS[o_ps]], tp=(0, par * 64) if par else None)
                            for kb in range(2):
                                mm(psum[d_ps][par * 64:(par + 1) * 64, cl * 128:(cl + 1) * 128], onesb[:, 0:64],
                                   pT[kb][:, (par * 2 + cl) * 128:(par * 2 + cl + 1) * 128], kb == 0, kb == 1, [B_ident, B_pT[kb]], [PS[d_ps]], tp=(0, par * 64) if par else None)
                            stop_if("s")
                        for par in range(2):
                            for cl in range(2):
                                qh = kvh * 4 + cl * 2 + par
                                K.op("dve", "tensor_scalar", dict(out=rden[par * 64:(par + 1) * 64, cl * 128:(cl + 1) * 128],
                                                                                         in0=psum[d_ps][par * 64:(par + 1) * 64, cl * 128:(cl + 1) * 128],
                                                                                         scalar1=esink[par * 64:(par + 1) * 64, qh:qh + 1], scalar2=None, op0=ALU.add),
                                     reads=[PS[d_ps], B_small], writes=[B_rden])
                        K.op("dve", "reciprocal", dict(out=rden[:, 0:256], in_=rden[:, 0:256]), reads=[B_rden], writes=[B_rden])
                        stop_if("s")
                        for cl in range(2):
                            ch = H + 2 * kvh + cl
                            K.op("dve", "tensor_tensor", dict(out=mixT[:, ch, jq * 128:(jq + 1) * 128], in0=psum[o_ps][:, cl * 128:(cl + 1) * 128], in1=rden[:, cl * 128:(cl + 1) * 128], op=ALU.mult),
                                 reads=[PS[o_ps], B_rden], writes=[B_mix[ch]])
                            K.op("act", "activation", dict(out=sqB[:, cl * 128:(cl + 1) * 128], in_=mixT[:, ch, jq * 128:(jq + 1) * 128], func=AF.Square),
                                 reads=[B_mix[ch]], writes=[B_sqB])
                            mm(psum[STB][:, jq * 128:(jq + 1) * 128], onesb, sqB[:, cl * 128:(cl + 1) * 128], cnt_statB[jq] == 0, cnt_statB[jq] == SQ // 2 - 1, [B_sqB, B_ident], [PS[STB]])
                            cnt_statB[jq] += 1
                            stop_if("s")
                rstdB = sreg(o1 + 11 * KB, 2 * KB, F32); B_rstdB = Buf("rstdB")
                rstd_from(rstdB, psum[STB][:, :], SQ * 64, [PS[STB]], [B_rstdB])
                for c in range(SQ // 2):
                    ch = H + c
                    K.op("dve", "scalar_tensor_tensor", dict(out=mixT[:, ch, :], in0=mixT[:, ch, :], scalar=gcol[:, KQ + KC + ch:KQ + KC + ch + 1], in1=rstdB, op0=ALU.mult, op1=ALU.mult),
                         reads=[B_mix[ch], B_rstdB, B_small], writes=[B_mix[ch]])
                if t == 0:
                    stop_if("swa0")
                K.barrier()
                o2 = o_qs
                nkeys = NO + (t + 1) * T
                nkb = nkeys // 128
                qhT = sreg(o2, 2 * T * 2, BF16).rearrange("p (k n) -> p k n", k=2); B_qh = Buf("qhT")
                khT = sreg(o2 + 2 * KB, NK * 2, BF16); B_kh = Buf("khT")
                vh = sreg(o2 + 2 * KB + NK * 2, NK * 2, BF16).rearrange("p (b d) -> p b d", d=128); B_vh = Buf("vh")
                o3 = o2 + 2 * KB + 2 * NK * 2
                pTm = [sreg(o3 + i * KB, KB, BF16) for i in range(3)]; B_pTm = [Buf("pTm%d" % i) for i in range(3)]
                stm = [sreg(o3 + 3 * KB + i * 2 * KB, 2 * KB, F32) for i in range(2)]; B_stm = [Buf("stm0"), Buf("stm1")]
                rdm = sreg(o3 + 7 * KB, 2 * KB, F32); B_rdm = Buf("rdm")
                t1 = sreg(o3 + 9 * KB, 2 * KB, F32)[0:64, :]; t2 = sreg(o3 + 11 * KB, 2 * KB, F32)[0:64, :]; B_t12 = Buf("t12")
                sqA = sreg(o3 + 13 * KB, KB, BF16); B_sqA = Buf("sqA")
                rstdA = sreg(o3 + 14 * KB, 2 * KB, F32); B_rstdA = Buf("rstdA")
                cosq, sinq, B_cs = cosT, sinT, B_tab
                STA = 7
                pidx = [0]
                K.op("dve", "memset", dict(ap=qhT[64:128, 1, :], constant=0.0), writes=[B_qh])
                for hd in range(H):
                    slot, B_s = w_get(("head", t, hd))
                    wq = slot_view(slot, KQ, 256)
                    wkv = slot[:, KQ * 256:KQ * 256 + KC * 256].rearrange("p (k c) -> p k c", k=KC)
                    for k in range(KQ):
                        mm(psum[4][:, :], wq[:, k, 0:128], outs["cqT"][:, k, :], k == 0, k == KQ - 1, [B_s, outs["B_cq"]], [PS[4]])
                    for k in range(KQ):
                        mm(psum[5][0:64, :], wq[:, k, 128:192], outs["cqT"][:, k, :], k == 0, k == KQ - 1, [B_s, outs["B_cq"]], [PS[5]])
                    for k in range(KQ):
                        mm(psum[6][0:64, :], wq[:, k, 192:256], outs["cqT"][:, k, :], k == 0, k == KQ - 1, [B_s, outs["B_cq"]], [PS[6]])
                    copy_any(qhT[:, 0, :], psum[4][:, :], [PS[4]], [B_qh], eng="act")
                    K.op("dve", "tensor_tensor", dict(out=t1, in0=psum[5][0:64, :], in1=cosq, op=ALU.mult), reads=[PS[5], B_cs], writes=[B_t12])
                    K.op("dve", "tensor_tensor", dict(out=t2, in0=psum[6][0:64, :], in1=sinq, op=ALU.mult), reads=[PS[6], B_cs], writes=[B_t12])
                    K.op("dve", "tensor_tensor", dict(out=qhT[0:64, 1, :], in0=t1, in1=t2, op=ALU.add), reads=[B_t12], writes=[B_qh])
                    for kc in range(nkeys // 512):
                        pi = 4 + kc % 3
                        for k in range(KC):
                            mm(psum[pi][:, :], wkv[:, k, 0:128], ckvT[:, k, kc * 512:(kc + 1) * 512], k == 0, k == KC - 1, [B_s, B_ckv[kc]], [PS[pi]])
                        copy_any(khT[:, kc * 512:(kc + 1) * 512], psum[pi][:, :], [PS[pi]], [B_kh])
                    for kb4 in range(nkb // 4):
                        pi = 4 + kb4 % 3
                        for j in range(4):
                            kb = kb4 * 4 + j
                            for k in range(KC):
                                mm(psum[pi][:, j * 128:(j + 1) * 128], ckvT[:, k, kb * 128:(kb + 1) * 128], wkv[:, k, 128:256], k == 0, k == KC - 1, [B_s, B_ckv[kb // 4]], [PS[pi]])
                        copy_any(vh[:, kb4 * 4:(kb4 + 1) * 4, :], psum[pi][:, :].rearrange("p (b d) -> p b d", d=128), [PS[pi]], [B_vh])
                    O_PS, D_PS = 2, 3
                    def s_mm(kb):
                        s_ps = kb % 2
                        mm(psum[s_ps][:, :], khT[:, kb * 128:(kb + 1) * 128], qhT[:, 0, :], True, False, [B_kh, B_qh], [PS[s_ps]])
                        mm(psum[s_ps][:, :], kpeT[:, kb * 128:(kb + 1) * 128], qhT[:, 1, :], False, True, [B_kpe[kb // 4], B_qh], [PS[s_ps]])
                    s_mm(0)
                    for kb in range(nkb):
                        s_ps = kb % 2
                        if kb + 1 < nkb:
                            s_mm(kb + 1)
                        pp = pidx[0] % 3
                        pidx[0] += 1
                        if kb < NO // 128:
                            K.op("act", "activation", dict(out=pTm[pp], in_=psum[s_ps][:, :], func=AF.Exp, scale=sc_mla, bias=ccol[:, 2:3]),
                                 reads=[PS[s_ps], B_cst], writes=[B_pTm[pp]])
                        elif kb < nkb - 4:
                            K.op("act", "activation", dict(out=pTm[pp], in_=psum[s_ps][:, :], func=AF.Exp, scale=sc_mla),
                                 reads=[PS[s_ps]], writes=[B_pTm[pp]])
                        else:
                            o = (kb - (nkb - 4)) * 128
                            si = kb % 2
                            K.op("dve", "scalar_tensor_tensor", dict(out=stm[si], in0=psum[s_ps][:, :], scalar=sc_mla, in1=cmask[:, 384 - o:384 - o + 512], op0=ALU.mult, op1=ALU.add),
                                 reads=[PS[s_ps], B_cst], writes=[B_stm[si]])
                            K.op("act", "activation", dict(out=pTm[pp], in_=stm[si], func=AF.Exp), reads=[B_stm[si]], writes=[B_pTm[pp]])
                        mm(psum[O_PS][:, :], vh[:, kb, :], pTm[pp], kb == 0, kb == nkb - 1, [B_vh, B_pTm[pp]], [PS[O_PS]])
                        mm(psum[D_PS][:, :], onesb, pTm[pp], kb == 0, kb == nkb - 1, [B_ident, B_pTm[pp]], [PS[D_PS]])
                    K.op("dve", "reciprocal", dict(out=rdm, in_=psum[D_PS][:, :]), reads=[PS[D_PS]], writes=[B_rdm])
                    K.op("dve", "tensor_tensor", dict(out=mixT[:, hd, :], in0=psum[O_PS][:, :], in1=rdm, op=ALU.mult), reads=[PS[O_PS], B_rdm], writes=[B_mix[hd]])
                    K.op("act", "activation", dict(out=sqA, in_=mixT[:, hd, :], func=AF.Square), reads=[B_mix[hd]], writes=[B_sqA])
                    mm(psum[STA][:, :], onesb, sqA, hd == 0, hd == H - 1, [B_sqA, B_ident], [PS[STA]])
                rstd_from(rstdA, psum[STA][:, :], H * 128, [PS[STA]], [B_rstdA])
                for hd in range(H):
                    K.op("dve", "scalar_tensor_tensor", dict(out=mixT[:, hd, :], in0=mixT[:, hd, :], scalar=gcol[:, KQ + KC + hd:KQ + KC + hd + 1], in1=rstdA, op0=ALU.mult, op1=ALU.mult),
                         reads=[B_mix[hd], B_rstdA, B_small], writes=[B_mix[hd]])
                if t == 0:
                    stop_if("mla0")
                K.barrier()
                o_y = KM * T * 2
                y_bf = sreg(o_y, NB * D * 2, BF16).rearrange("p (b d) -> p b d", b=NB); B_y = Buf("y")
                NPW = D // 256
                sspart = ssp_cells; B_ssp = Buf("ssp")
                junk = sreg(o_y + NB * D * 2, KB, F32)[:, 0:256]; B_junk = Buf("junk")
                for cb in range(NPW):
                    slot, B_s = w_get(("wo", t, cb))
                    sv = slot_view(slot, KM, 256)
                    for blk in range(NB):
                        pi = (cb * NB + blk) % 4
                        for k in range(KM):
                            mm(psum[pi][:, 0:256], mixT[:, k, blk * 128:(blk + 1) * 128], sv[:, k, :], k == 0, k == KM - 1, [B_s, B_mix[k]], [PS[pi]])
                        copy_any(y_bf[:, blk, cb * 256:(cb + 1) * 256], psum[pi][:, 0:256], [PS[pi]], [B_y])
                        K.op("act", "activation", dict(out=junk, in_=y_bf[:, blk, cb * 256:(cb + 1) * 256], func=AF.Square, accum_out=sspart[:, blk * NPW + cb:blk * NPW + cb + 1]),
                             reads=[B_y], writes=[B_junk, B_ssp])
                K.barrier()
                postnorm(y_bf, B_y, sspart, B_ssp, NPW, 1,
                         lambda blk, t=t: x_own[t * T + blk * 128:t * T + (blk + 1) * 128, :], [],
                         lambda blk, t=t: out[t * T + blk * 128:t * T + (blk + 1) * 128, :], B_out, 0, o_y + NB * D * 2 + KB)

            stop_if("h")
            K.barrier()
            assert KD * T * 2 <= sz_ckv + NK * 2 or True
            for t in range(NTILE):
                K.barrier()
                fT = region(o_ckv, KD * T * 2, BF16).rearrange("p (k n) -> p k n", k=KD); B_fT = Buf("fT")
                norm_transpose(lambda blk, t=t: out[t * T + blk * 128:t * T + (blk + 1) * 128, :], B_out, 2, fT, B_fT, 0, 2 * D * 4, 3 * D * 4)
                K.barrier()
                actT = sreg(0, KF * T * 2, BF16).rearrange("p (k n) -> p k n", k=KF)
                B_act = [Buf("act%d" % j) for j in range(KF)]
                sgt = [sreg(KF * T * 2 + i * 2 * KB, 2 * KB, F32) for i in range(2)]
                B_sg = [Buf("sg0"), Buf("sg1")]
                for j in range(KF):
                    slot, B_s = w_get(("gu", t, j))
                    wg = slot[:, 0:KD * 128].rearrange("p (k c) -> p k c", k=KD)
                    wu = slot[:, KD * 128:2 * KD * 128].rearrange("p (k c) -> p k c", k=KD)
                    pg, pu = (j % 2) * 2, (j % 2) * 2 + 1
                    for k in range(KD):
                        mm(psum[pg][:, :], wg[:, k, :], fT[:, k, :], k == 0, k == KD - 1, [B_s, B_fT], [PS[pg]])
                    for k in range(KD):
                        mm(psum[pu][:, :], wu[:, k, :], fT[:, k, :], k == 0, k == KD - 1, [B_s, B_fT], [PS[pu]])
                    K.op("act", "activation", dict(out=sgt[j % 2], in_=psum[pg][:, :], func=AF.Silu), reads=[PS[pg]], writes=[B_sg[j % 2]])
                    K.op("dve", "tensor_tensor", dict(out=actT[:, j, :], in0=psum[pu][:, :], in1=sgt[j % 2], op=ALU.mult), reads=[PS[pu], B_sg[j % 2]], writes=[B_act[j]])
                K.barrier()
                y_bf = region(o_ckv, NB * D * 2, BF16).rearrange("p (b d) -> p b d", b=NB); B_y = Buf("y2")
                NPD = D // 512
                sspart = ssp_cells; B_ssp = Buf("ssp2")
                junk = region(o_cs, T * 2, BF16); B_junk = Buf("junk2")
                KG = SLOT // 2 // 512
                for cb in range(NPD):
                    banks = [4 + b for b in range(NB)] if cb % 2 == 0 else [b for b in range(NB)]
                    for k0 in range(0, KF, KG):
                        kn = min(KG, KF - k0)
                        slot, B_s = w_get(("dn", t, cb, k0))
                        sv = slot_view(slot, kn, 512)
                        for kk in range(kn):
                            k = k0 + kk
                            for blk in range(NB):
                                mm(psum[banks[blk]][:, :], actT[:, k, blk * 128:(blk + 1) * 128], sv[:, kk, :], k == 0, k == KF - 1, [B_s, B_act[k]], [PS[banks[blk]]])
                    for blk in range(NB):
                        pi = banks[blk]
                        copy_any(y_bf[:, blk, cb * 512:(cb + 1) * 512], psum[pi][:, :], [PS[pi]], [B_y])
                        K.op("act", "activation", dict(out=junk, in_=y_bf[:, blk, cb * 512:(cb + 1) * 512], func=AF.Square, accum_out=sspart[:, blk * NPD + cb:blk * NPD + cb + 1]),
                             reads=[B_y], writes=[B_junk, B_ssp])
                K.barrier()
                postnorm(y_bf, B_y, sspart, B_ssp, NPD, 3,
                         lambda blk, t=t: out[t * T + blk * 128:t * T + (blk + 1) * 128, :], [B_out],
                         lambda blk, t=t: out[t * T + blk * 128:t * T + (blk + 1) * 128, :], B_out, 0, 2 * D * 4)

        try:
            build_phases()
            assert wstate["cur"] == len(wplan), (wstate["cur"], len(wplan))
        except StopBuild:
            print("STOPPED at", STOP)
            for b in B_slot:
                for tk in b.w.values():
                    K.engs["sp"].wait(*tk)

        e = K.engs["sp"]
        for sem, val in K.store_tickets:
            e.wait(sem, val)
        n_ins = {n: len(e.q) for n, e in K.engs.items()}
        print("instruction counts", n_ins)
        with nc.Block() as block:
            K.emit(block)
    return nc


def host_prep(inp, P):
    D, NO, H, SQ, SKV, QR, KVR = P["D"], P["NO"], P["H"], P["SQ"], P["SKV"], P["QR"], P["KVR"]
    KQ, KC, KM = P["KQ"], P["KC"], P["KM"]
    f32 = np.float32
    x = np.asarray(inp["x"], f32)
    pos = np.asarray(inp["positions"]).astype(np.int32)
    w_in = np.asarray(inp["w_in"], f32)[0]
    sp = np.cumsum([QR, KVR, 64, SQ * 64, SKV * 64, SKV * 64])
    cq, ckv, kr, qs, ks, vs = np.split(w_in, sp[:-1], axis=1)
    kdup = ks.reshape(D, SKV, 1, 64).repeat(2, axis=2).reshape(D, SKV * 128)
    kr_sw = np.concatenate([kr[:, 32:], kr[:, :32]], axis=1)
    w_in2 = np.concatenate([cq, ckv, qs, kdup, vs, kr, kr_sw], axis=1)
    assert w_in2.shape[1] == P["CIN"]
    KD, KF, DFF = P["KD"], P["KF"], P["DFF"]
    w3 = w_in2.reshape(KD, 128, P["CIN"])
    blks = []
    for (c0, n) in P["_in_blocks"]:
        blks.append(np.ascontiguousarray(w3[:, :, c0:c0 + n].transpose(1, 0, 2)).reshape(128, KD * n))
    w_in_r = np.ascontiguousarray(np.concatenate(blks, axis=1))
    wuq = np.asarray(inp["w_uq"], f32)[0].reshape(QR, H, 192)
    wuq2 = np.concatenate([wuq, wuq[:, :, 160:192], wuq[:, :, 128:160]], axis=2)
    wuq_r = np.ascontiguousarray(wuq2.reshape(KQ, 128, H, 256).transpose(1, 2, 0, 3)).reshape(128, H * KQ * 256)
    wukv = np.asarray(inp["w_ukv"], f32)[0]
    wukv_r = np.ascontiguousarray(wukv.reshape(KC, 128, H, 256).transpose(1, 2, 0, 3)).reshape(128, H * KC * 256)
    w_o = np.asarray(inp["w_o"], f32)[0]
    w_o_r = np.ascontiguousarray(w_o.reshape(KM, 128, D // 256, 256).transpose(1, 2, 0, 3)).reshape(128, -1)
    w_gate_r = np.ascontiguousarray(np.asarray(inp["w_gate"], f32)[0].reshape(KD, 128, KF, 128).transpose(1, 2, 0, 3)).reshape(128, -1)
    w_up_r = np.ascontiguousarray(np.asarray(inp["w_up"], f32)[0].reshape(KD, 128, KF, 128).transpose(1, 2, 0, 3)).reshape(128, -1)
    w_down_r = np.ascontiguousarray(np.asarray(inp["w_down"], f32)[0].reshape(KF, 128, D // 512, 512).transpose(1, 2, 0, 3)).reshape(128, -1)
    g_rows = np.ascontiguousarray(np.stack([np.asarray(inp[k], f32)[0] for k in ("attn_pre_g", "attn_post_g", "ffn_pre_g", "ffn_post_g")]))
    col = lambda v, n: np.asarray(v, f32)[0].reshape(n, 128).T
    g_cols = np.ascontiguousarray(np.concatenate([col(inp["q_norm_g"], KQ), col(inp["kv_norm_g"], KC),
                                                  col(inp["grp_a_g"], H), col(inp["grp_b_g"], SQ // 2)], axis=1))
    sinks = np.ascontiguousarray(np.asarray(inp["swa_sinks"], f32).reshape(1, SQ))
    cst = np.zeros((2, 128, 128 + 896 + 128 + 8), f32)
    kk = np.arange(128)[:, None]
    cst[:, :, 0:128] = np.eye(128, dtype=f32)
    cc = np.arange(896)[None, :]
    cst[:, :, 128:1024] = np.where(cc - 384 >= kk, 0.0, NEG)
    qq = np.arange(128)[None, :]
    cst[:, :, 1024:1152] = np.where(qq < kk, 0.0, NEG)
    inv = (1.0 / (10000.0 ** (np.arange(0, 64, 2, dtype=np.float32) / 64.0))).astype(f32)
    cst[:, 0:64, 1152] = np.concatenate([inv, inv])
    cst[:, 0:64, 1153] = np.concatenate([np.full(32, math.pi), np.full(32, 2 * math.pi)])
    cst[0, :, 1154] = NEG
    cst[1, :, 1154] = 0.0
    cst[:, :, 1155] = -math.pi
    cst[:, :, 1156] = EPS
    cst[:, :, 1157] = 2.5 * math.pi
    maps = []
    for core in range(P["NCORE"]):
        b, hf = core // 2, core % 2
        maps.append({
            "x_own": np.ascontiguousarray(x[b, hf * NO:(hf + 1) * NO]),
            "x_ctx": np.ascontiguousarray(x[b, 0:NO]),
            "pos_own": np.ascontiguousarray(pos[b, hf * NO:(hf + 1) * NO].reshape(1, NO)),
            "pos_ctx": np.ascontiguousarray(pos[b, 0:NO].reshape(1, NO)),
            "w_in": w_in_r, "w_uq": wuq_r, "w_ukv": wukv_r, "w_o": w_o_r, "w_gate": w_gate_r, "w_up": w_up_r, "w_down": w_down_r,
            "g_rows": g_rows, "g_cols": g_cols, "sinks": sinks, "cst": np.ascontiguousarray(cst[hf]),
        })
    return maps


def run(inp, P, trace=False):
    P = derive(P)
    nc = build(P)
    maps = host_prep(inp, P)
    res = run_bass_kernel_spmd(nc, maps, core_ids=list(range(P["NCORE"])), **({"trace": True} if trace else {}))
    NO = P["NO"]
    outp = np.zeros((P["B"], P["SEQ"], P["D"]), np.float32)
    for core in range(P["NCORE"]):
        b, hf = core // 2, core % 2
        outp[b, hf * NO:(hf + 1) * NO] = res.results[core]["out"]
    return outp, res


def kernel(**inputs):
    outp, _ = run(inputs, FULL)
    return outp
```

```python
import math
from contextlib import ExitStack
import numpy as np
import concourse.bass as bass
import concourse.mybir as mybir
from concourse.bass_utils import run_bass_kernel_spmd

F32 = mybir.dt.float32
BF16 = mybir.dt.bfloat16
I32 = mybir.dt.int32
AF = mybir.ActivationFunctionType
ALU = mybir.AluOpType
AX = mybir.AxisListType

NEG = -30000.0
EPS = 1e-6
SAME_ENGINE_WAITS = True

FULL = dict(D=4096, B=4, SEQ=4096, H=16, QR=1024, KVR=512, SQ=32, SKV=8, DFF=11008, T=512)


def derive(P):
    P = dict(P)
    P["NO"] = P["SEQ"] // 2
    P["KD"] = P["D"] // 128
    P["KQ"] = P["QR"] // 128
    P["KC"] = P["KVR"] // 128
    P["KF"] = P["DFF"] // 128
    P["MIXA"] = P["H"] * 128
    P["MIXB"] = P["SQ"] * 64
    P["KM"] = (P["MIXA"] + P["MIXB"]) // 128
    P["NTILE"] = P["NO"] // P["T"]
    P["CIN"] = P["QR"] + P["KVR"] + P["SQ"] * 64 + P["SKV"] * 128 + P["SKV"] * 64 + 128
    P["NCORE"] = 2 * P["B"]
    blks = []
    def rng(c0, n):
        for i in range(0, n, 256):
            blks.append((c0 + i, min(256, n - i)))
    c = 0
    for n in (P["QR"], P["KVR"], P["SQ"] * 64, P["SKV"] * 128, P["SKV"] * 64, 128):
        rng(c, n)
        c += n
    P["_in_blocks"] = blks
    return P


class Eng:
    def __init__(self, name, sem):
        self.name, self.sem = name, sem
        self.q = []
        self.cnt = 0
        self.seen = {}

    def wait(self, sem, val):
        if val <= 0:
            return
        if sem is self.sem and (not SAME_ENGINE_WAITS or self.name in ("pe", "sp", "pool")):
            return
        if self.seen.get(id(sem), 0) < val:
            self.q.append(("wait", sem, val))
            self.seen[id(sem)] = val


class Buf:
    def __init__(self, name):
        self.name = name
        self.w = {}
        self.r = {}
        self.dsem = None
        self.dcnt = 0
        self.excl = name.startswith("ps") and name[2:].isdigit()


class StopBuild(Exception):
    pass


class Ker:
    def __init__(self, nc, stack):
        self.nc = nc
        self.stack = stack
        self.engs = {}
        for n in ("pe", "act", "dve", "pool", "sp"):
            self.engs[n] = Eng(n, stack.enter_context(nc.semaphore("s_" + n)))
        self.store_tickets = []

    def newsem(self, name):
        return self.stack.enter_context(self.nc.semaphore(name))

    def _deps(self, e, reads, writes):
        need = {}
        def add(t):
            sem, val = t
            k = id(sem)
            if k not in need or need[k][1] < val:
                need[k] = (sem, val)
        for b in reads:
            for t in b.w.values():
                add(t)
            if b.excl:
                for t in b.r.values():
                    add(t)
        for b in writes:
            for t in b.w.values():
                add(t)
            for t in b.r.values():
                add(t)
        for sem, val in need.values():
            e.wait(sem, val)

    def op(self, en, meth, kw, reads=(), writes=()):
        fn = (meth, kw)
        e = self.engs[en]
        self._deps(e, reads, writes)
        e.cnt += 1
        t = (e.sem, e.cnt)
        for b in reads:
            b.r[en] = t
        for b in writes:
            b.w[en] = t
            b.r = {}
        e.q.append(("op", fn, e.sem, 1))

    def dma(self, en, meth, kw, reads=(), writes=(), sembuf=None):
        fn = (meth, kw)
        e = self.engs[en]
        self._deps(e, reads, writes)
        sb = sembuf if sembuf is not None else (writes[0] if writes else reads[0])
        if sb.dsem is None:
            sb.dsem = self.newsem("d_" + sb.name)
        sb.dcnt += 1
        t = (sb.dsem, 16 * sb.dcnt)
        for b in reads:
            b.r["dma_" + sb.name] = t
        for b in writes:
            b.w["dma_" + sb.name] = t
            b.r = {}
        e.q.append(("op", fn, sb.dsem, 16))
        return t

    def barrier(self, names=("pe", "act", "dve", "sp")):
        snap = [(self.engs[n].sem, self.engs[n].cnt) for n in names]
        for n in names:
            e = self.engs[n]
            for sem, val in snap:
                if sem is not e.sem:
                    e.wait(sem, val)
            for sem, val in self.store_tickets:
                e.wait(sem, val)

    def emit(self, block):
        nc = self.nc
        hmap = {"pe": "tensor", "act": "scalar", "dve": "vector", "pool": "gpsimd", "sp": "sync"}
        for n, e in self.engs.items():
            def body(h, e=e):
                for it in e.q:
                    if it[0] == "wait":
                        h.wait_ge(it[1], it[2])
                    else:
                        getattr(h, it[1][0])(**it[1][1]).then_inc(it[2], it[3])
            getattr(block, hmap[n])(body)


def build(P):
    D, NO, T, KD, KQ, KC, KF, KM = P["D"], P["NO"], P["T"], P["KD"], P["KQ"], P["KC"], P["KF"], P["KM"]
    H, SQ, SKV, QR, KVR, DFF, CIN = P["H"], P["SQ"], P["SKV"], P["QR"], P["KVR"], P["DFF"], P["CIN"]
    NTILE = P["NTILE"]
    NB = T // 128
    NK = 2 * NO
    KMAX = max(KD, KM)
    assert T == 512 and KF * 128 == DFF

    nc = bass.Bass("TRN2", target_bir_lowering=False)
    dt_in = lambda n, s, d=F32: nc.dram_tensor(n, s, d, kind="ExternalInput").ap()
    x_own = dt_in("x_own", [NO, D])
    x_ctx = dt_in("x_ctx", [NO, D])
    pos_own = dt_in("pos_own", [1, NO], I32)
    pos_ctx = dt_in("pos_ctx", [1, NO], I32)
    w_in = dt_in("w_in", [128, KD * CIN])
    w_uq = dt_in("w_uq", [128, H * KQ * 256])
    w_ukv = dt_in("w_ukv", [128, H * KC * 256])
    w_o = dt_in("w_o", [128, (D // 256) * KM * 256])
    w_gate = dt_in("w_gate", [128, KF * KD * 128])
    w_up = dt_in("w_up", [128, KF * KD * 128])
    w_down = dt_in("w_down", [128, (D // 512) * KF * 512])
    g_rows = dt_in("g_rows", [4, D])
    NCOL = KQ + KC + KM
    g_cols = dt_in("g_cols", [128, NCOL])
    sinks = dt_in("sinks", [1, SQ])
    NCST = 128 + 896 + 128 + 8
    cst = dt_in("cst", [128, NCST])
    out = nc.dram_tensor("out", [NO, D], F32, kind="ExternalOutput").ap()

    stack = ExitStack()
    with stack:
        K = Ker(nc, stack)
        ARENA = 192 * 1024
        arena = stack.enter_context(nc.sbuf_tensor("arena", [128, ARENA // 2], BF16))
        psum = [stack.enter_context(nc.psum_tensor("ps%d" % i, [128, 512], F32)) for i in range(8)]
        PS = [Buf("ps%d" % i) for i in range(8)]

        def region(off, nbytes, dtype, shape=None):
            assert off % 4 == 0 and off + nbytes <= ARENA, (off, nbytes)
            a = arena[:, off // 2:(off + nbytes) // 2]
            if dtype is F32 or dtype is I32:
                a = a.bitcast(dtype)
            return a

        KB = 1024
        o_cst = 0
        o_ckv = 5 * KB
        sz_ckv = KC * NK * 2
        o_kpe = o_ckv + sz_ckv
        o_slots = o_kpe + NK * 2
        SLOT = KMAX * 256 * 2
        NSLOT = 3
        o_carry = o_slots + NSLOT * SLOT
        o_small = o_carry + SKV * 128 * 2 + SKV * 64 * 2
        o_small = (o_small + 3) // 4 * 4
        o_cs = o_small + 2 * KB
        o_stage = o_cs + 2 * T * 4
        STAGE = ARENA - o_stage
        print("arena: stage offset", o_stage, "stage bytes", STAGE)

        cst_t = region(o_cst, NCST * 4, F32)
        ident_f = cst_t[:, 0:128]
        cmask = cst_t[:, 128:128 + 896]
        pmask = cst_t[:, 1024:1152]
        ccol = cst_t[:, 1152:1160]
        B_cst = Buf("cst")
        small = region(o_small, 2 * KB, F32)
        B_small = Buf("small")
        gcol = small[:, 0:NCOL]
        esink = small[:, 64:64 + SQ]
        small_next = [64 + SQ]
        def cell(n=1):
            a = small[:, small_next[0]:small_next[0] + n]
            small_next[0] += n
            assert small_next[0] <= 512 - 128
            return a
        ss_cells, rs_cells, ssp_cells = cell(2 * NB), cell(2 * NB), cell(NB * (D // 256))
        G_xs = [Buf("gxs0"), Buf("gxs1")]
        G_g, G_pos, G_tmp = Buf("gg"), Buf("gpos"), Buf("gtmp")
        identb = region(o_small + 2 * KB - 512, 256, BF16)
        onesb = region(o_small + 2 * KB - 256, 256, BF16)
        B_ident = Buf("ident")

        ckvT = region(o_ckv, sz_ckv, BF16).rearrange("p (k n) -> p k n", k=KC)
        B_ckv = [Buf("ckv%d" % i) for i in range(NK // 512)]
        kpeT = region(o_kpe, NK * 2, BF16)
        B_kpe = [Buf("kpe%d" % i) for i in range(NK // 512)]
        slots = [region(o_slots + i * SLOT, SLOT, BF16) for i in range(NSLOT)]
        B_slot = [Buf("slot%d" % i) for i in range(NSLOT)]
        kd_prev = region(o_carry, SKV * 128 * 2, BF16).rearrange("p (h n) -> p h n", h=SKV)
        v_prev = region(o_carry + SKV * 128 * 2, SKV * 64 * 2, BF16)
        B_carry = Buf("carry")

        def sreg(off, nbytes, dtype):
            assert off + nbytes <= STAGE, ("stage overflow", off, nbytes, STAGE)
            return region(o_stage + off, nbytes, dtype)

        wplan = []
        wstate = {"issued": 0, "cur": 0}

        def w_issue_upto(i):
            while wstate["issued"] <= min(i, len(wplan) - 1):
                j = wstate["issued"]
                key, parts = wplan[j]
                s = j % NSLOT
                for (dstf, src) in parts:
                    dst = dstf(slots[s])
                    K.dma("pool", "dma_start", dict(out=dst, in_=src),
                          writes=[B_slot[s]])
                wstate["issued"] += 1

        def w_get(key):
            j = wstate["cur"]
            assert wplan[j][0] == key, (wplan[j][0], key)
            w_issue_upto(j + NSLOT - 1)
            wstate["cur"] += 1
            return slots[j % NSLOT], B_slot[j % NSLOT]

        def slot_view(slot, kk, cols):
            return slot[:, 0:kk * cols].rearrange("p (k c) -> p k c", k=kk)

        def chunked(ap, n):
            b_ = 2048
            while n % b_:
                b_ //= 2
            return ap.rearrange("p (a b) -> p a b", b=b_)

        c_q0 = 0
        c_kv0 = QR
        c_qs0 = QR + KVR
        c_kd0 = c_qs0 + SQ * 64
        c_vs0 = c_kd0 + SKV * 128
        c_kr0 = c_vs0 + SKV * 64
        def inproj_blocks(kind):
            blks = []
            def rng(name, c0, n):
                for i in range(0, n, 256):
                    blks.append((name, i // 128, c0 + i, min(256, n - i)))
            if kind == "own":
                rng("cq", c_q0, QR)
            rng("ckv", c_kv0, KVR)
            if kind in ("own", "ctxlast"):
                if kind == "own":
                    rng("qs", c_qs0, SQ * 64)
                rng("kd", c_kd0, SKV * 128)
                rng("vs", c_vs0, SKV * 64)
            rng("kr", c_kr0, 128)
            return blks

        def plan():
            def flat(dram, off, n, soff=0):
                return (lambda s, n=n, soff=soff: chunked(s[:, soff:soff + n], n), chunked(dram[:, off:off + n], n))
            in_off = {}
            o = 0
            for (name, mc, c0, n) in inproj_blocks("own"):
                in_off[(name, mc)] = o
                o += KD * n
            assert o == KD * CIN
            for t in range(NTILE):
                kind = "ctxlast" if t == NTILE - 1 else "ctx"
                for (name, mc, c0, n) in inproj_blocks(kind):
                    wplan.append((("in", "c", t, name, mc), [flat(w_in, in_off[(name, mc)], KD * n)]))
            for t in range(NTILE):
                for (name, mc, c0, n) in inproj_blocks("own"):
                    wplan.append((("in", "o", t, name, mc), [flat(w_in, in_off[(name, mc)], KD * n)]))
                for h in range(H):
                    wplan.append((("head", t, h), [flat(w_uq, h * KQ * 256, KQ * 256),
                                                   flat(w_ukv, h * KC * 256, KC * 256, KQ * 256)]))
                for cb in range(D // 256):
                    wplan.append((("wo", t, cb), [flat(w_o, cb * KM * 256, KM * 256)]))
            for t in range(NTILE):
                for j in range(KF):
                    wplan.append((("gu", t, j), [flat(w_gate, j * KD * 128, KD * 128),
                                                 flat(w_up, j * KD * 128, KD * 128, KD * 128)]))
                KG = SLOT // 2 // 512
                for cb in range(D // 512):
                    for k0 in range(0, KF, KG):
                        kn = min(KG, KF - k0)
                        wplan.append((("dn", t, cb, k0), [flat(w_down, (cb * KF + k0) * 512, kn * 512)]))
        plan()
        assert KQ * 256 + KC * 256 <= SLOT // 2

        def mm(out_ap, lhsT, rhs, start, stop, reads, writes, tp=None):
            kw = {}
            if tp is not None:
                kw["tile_position"] = tp
            K.op("pe", "matmul", dict(out=out_ap, lhsT=lhsT, rhs=rhs, start=start, stop=stop, **kw), reads=reads, writes=writes)

        evac_rr = [0]
        def copy_any(out_ap, in_ap, reads, writes, eng=None):
            if eng is None:
                eng = ("act", "dve")[evac_rr[0] % 2]
                evac_rr[0] += 1
            if eng == "act":
                K.op("act", "activation", dict(out=out_ap, in_=in_ap, func=AF.Copy), reads=reads, writes=writes)
            else:
                K.op(eng, "tensor_copy", dict(out=out_ap, in_=in_ap), reads=reads, writes=writes)

        def rstd_from(dst, src, n_feat, reads, writes):
            K.op("act", "activation", dict(out=dst, in_=src, func=AF.Sqrt, scale=1.0 / n_feat, bias=ccol[:dst.shape[0], 4:5]),
                 reads=list(reads) + [B_cst], writes=writes)
            K.op("dve", "reciprocal", dict(out=dst, in_=dst), reads=writes, writes=writes)

        K.dma("sp", "dma_start", dict(out=cst_t, in_=cst), writes=[B_cst])
        K.dma("sp", "dma_start", dict(out=gcol, in_=g_cols), writes=[B_small])
        K.dma("sp", "dma_start", dict(out=esink, in_=sinks.partition_broadcast(128)), writes=[B_small])
        K.op("act", "activation", dict(out=esink, in_=esink, func=AF.Exp), reads=[B_small], writes=[B_small])
        K.op("dve", "tensor_copy", dict(out=identb, in_=ident_f), reads=[B_cst], writes=[B_ident])
        K.op("dve", "memset", dict(ap=onesb, constant=1.0), writes=[B_ident])
        K.op("dve", "memset", dict(ap=kpeT[64:128, :], constant=0.0), writes=B_kpe)

        ps_rr = {"n": 0}

        def norm_transpose(src_rows_fn, B_src, grow, aT, B_aT, o_xs, o_g, o_abf):
            xs = [sreg(o_xs + i * D * 4, D * 4, F32) for i in range(2)]
            B_xs = G_xs
            g_bc = sreg(o_g, D * 4, F32)
            B_g = G_g
            abf = [sreg(o_abf + i * D * 2, D * 2, BF16) for i in range(1)]
            B_abf = [Buf("abf0")]
            ss = ss_cells
            B_ss = Buf("ss")
            K.dma("sp", "dma_start", dict(out=g_bc, in_=g_rows[grow:grow + 1, :].partition_broadcast(128)), writes=[B_g])
            for blk in range(NB):
                x_t, B_x = xs[blk % 2], B_xs[blk % 2]
                a_t, B_a = abf[0], B_abf[0]
                src = src_rows_fn(blk)
                K.dma("sp", "dma_start", dict(out=x_t, in_=src), reads=[B_src] if B_src else [], writes=[B_x])
                s1 = ss[:, 2 * blk:2 * blk + 1]
                s2 = ss[:, 2 * blk + 1:2 * blk + 2]
                B_ssb = Buf("ss%d" % blk)
                K.op("act", "activation", dict(out=aT[:, :, blk * 128:(blk + 1) * 128], in_=x_t.rearrange("p (k n) -> p k n", k=KD), func=AF.Square, accum_out=s1),
                     reads=[B_x], writes=[B_ssb])
                rstd_from(s2, s1, D, [B_ssb], [B_ssb])
                K.op("dve", "scalar_tensor_tensor", dict(out=a_t, in0=x_t, scalar=s2, in1=g_bc, op0=ALU.mult, op1=ALU.mult),
                     reads=[B_x, B_ssb, B_g], writes=[B_a])
                for c0 in range(0, KD, 8):
                    cn = min(8, KD - c0)
                    pi = ps_rr["n"] % 2
                    ps_rr["n"] += 1
                    pb = psum[pi][:, :].bitcast(BF16)
                    for c in range(cn):
                        K.op("pe", "transpose", dict(out=pb[:, c * 128:(c + 1) * 128], in_=a_t[:, (c0 + c) * 128:(c0 + c + 1) * 128], identity=identb),
                             reads=[B_a, B_ident], writes=[PS[pi]])
                    copy_any(aT[:, c0:c0 + cn, blk * 128:(blk + 1) * 128], pb[:, 0:cn * 128].rearrange("p (c n) -> p c n", c=cn),
                             reads=[PS[pi]], writes=[B_aT])

        def rope_tables(pos_ap, cosT, sinT, tmp, tmpi, B_tab, B_tmp):
            K.dma("sp", "dma_start", dict(out=tmpi, in_=pos_ap.partition_broadcast(64)), writes=[B_tmp])
            K.op("dve", "tensor_copy", dict(out=tmp, in_=tmpi), reads=[B_tmp], writes=[B_tmp])
            for (dst, shcol) in ((cosT, ccol[0:64, 5:6]), (sinT, ccol[0:64, 1:2])):
                K.op("dve", "tensor_scalar", dict(out=dst, in0=tmp, scalar1=ccol[0:64, 0:1], scalar2=shcol, op0=ALU.mult, op1=ALU.add),
                     reads=[B_tmp, B_cst], writes=[B_tab])
                K.op("dve", "tensor_scalar", dict(out=tmpi, in0=dst, scalar1=1.0 / (2 * math.pi), scalar2=None, op0=ALU.mult),
                     reads=[B_tab], writes=[B_tmp])
                K.op("dve", "tensor_copy", dict(out=tmp2_holder[0], in_=tmpi), reads=[B_tmp], writes=[B_tmp])
                K.op("dve", "scalar_tensor_tensor", dict(out=dst, in0=tmp2_holder[0], scalar=-2 * math.pi, in1=dst, op0=ALU.mult, op1=ALU.add),
                     reads=[B_tmp, B_tab], writes=[B_tab])
                K.op("dve", "tensor_scalar", dict(out=tmp2_holder[0], in0=dst, scalar1=math.pi, scalar2=-2 * math.pi, op0=ALU.is_gt, op1=ALU.mult),
                     reads=[B_tab], writes=[B_tmp])
                K.op("dve", "tensor_tensor", dict(out=dst, in0=dst, in1=tmp2_holder[0], op=ALU.add), reads=[B_tab, B_tmp], writes=[B_tab])
                K.op("dve", "tensor_scalar", dict(out=tmp2_holder[0], in0=dst, scalar1=-math.pi, scalar2=2 * math.pi, op0=ALU.is_lt, op1=ALU.mult),
                     reads=[B_tab], writes=[B_tmp])
                K.op("dve", "tensor_tensor", dict(out=dst, in0=dst, in1=tmp2_holder[0], op=ALU.add), reads=[B_tab, B_tmp], writes=[B_tab])
                K.op("dve", "tensor_scalar", dict(out=dst, in0=dst, scalar1=-3.1415925, scalar2=3.1415925, op0=ALU.max, op1=ALU.min),
                     reads=[B_tab], writes=[B_tab])
                K.op("act", "activation", dict(out=dst, in_=dst, func=AF.Sin), reads=[B_tab], writes=[B_tab])
        tmp2_holder = [None]

        def inproj(kind, t, aT, B_aT, kt, pos_ap, o0, outs):
            sq = sreg(o0, T * 2, BF16); B_sq = Buf("sq")
            rstd = sreg(o0 + 1 * KB, T * 4, F32); B_rstd = Buf("rstd")
            cosT = region(o_cs, T * 4, F32)[0:64, :]
            sinT = region(o_cs + T * 4, T * 4, F32)[0:64, :]
            tmp = sreg(o0 + 3 * KB, T * 4, F32)[0:64, :]
            tmpi = sreg(o0 + 5 * KB, T * 4, I32)[0:64, :]
            tmp2_holder[0] = sreg(o0 + 7 * KB, T * 4, F32)[0:64, :]
            B_tab, B_tmp = Buf("tab"), G_tmp
            rope_tables(pos_ap, cosT, sinT, tmp, tmpi, B_tab, B_tmp)
            stop_if("rope")
            kc0 = kt * T
            stat = {}
            acc_rr = [0]
            def acc_bank():
                i = 2 + acc_rr[0] % 3
                acc_rr[0] += 1
                return i
            nchunks = {"cq": KQ, "ckv": KC}
            done = {"cq": 0, "ckv": 0}
            ck = "c" if kind != "own" else "o"
            for (name, mc, c0, n) in inproj_blocks(kind):
                slot, B_s = w_get(("in", ck, t, name, mc))
                sv = slot_view(slot, KD, n)
                stop_if("blk")
                if STOP == "w1":
                    K.op("dve", "tensor_copy", dict(out=kpeT[:, 0:64], in_=sv[:, 0, 0:64]), reads=[B_s], writes=[B_kpe[0]])
                    stop_if("w1")
                if name in ("cq", "ckv", "qs", "kd"):
                    for m in range(n // 128):
                        pi = acc_bank()
                        for k in range(KD):
                            mm(psum[pi][:, :], sv[:, k, m * 128:(m + 1) * 128], aT[:, k, :], k == 0, k == KD - 1,
                               [B_s, B_aT], [PS[pi]])
                        ch = mc + m
                        if name in ("cq", "ckv"):
                            if name == "cq":
                                dst, B_d, sp_i = outs["cqT"][:, ch, :], outs["B_cq"], 6
                            else:
                                dst, B_d, sp_i = ckvT[:, ch, kc0:kc0 + T], B_ckv[kt], 6
                            K.op("act", "activation", dict(out=dst, in_=psum[pi][:, :], func=AF.Copy), reads=[PS[pi]], writes=[B_d])
                            K.op("dve", "tensor_tensor", dict(out=sq, in0=dst, in1=dst, op=ALU.mult), reads=[B_d], writes=[B_sq])
                            mm(psum[sp_i][:, :], onesb, sq, done[name] == 0, done[name] == nchunks[name] - 1, [B_sq, B_ident], [PS[sp_i]])
                            done[name] += 1
                            if done[name] == nchunks[name]:
                                nf = QR if name == "cq" else KVR
                                goff = 0 if name == "cq" else KQ
                                rstd_from(rstd, psum[sp_i][:, :], nf, [PS[sp_i]], [B_rstd])
                                for c in range(nchunks[name]):
                                    d2 = outs["cqT"][:, c, :] if name == "cq" else ckvT[:, c, kc0:kc0 + T]
                                    K.op("dve", "scalar_tensor_tensor", dict(out=d2, in0=d2, scalar=gcol[:, goff + c:goff + c + 1], in1=rstd, op0=ALU.mult, op1=ALU.mult),
                                         reads=[B_d, B_rstd, B_small], writes=[B_d])
                        elif name == "qs":
                            copy_any(outs["qsT"][:, ch, :], psum[pi][:, :], [PS[pi]], [outs["B_qs"]])
                        else:
                            copy_any(outs["kdT"][:, ch, 128:128 + T], psum[pi][:, :], [PS[pi]], [outs["B_kd"]])
                elif name == "vs":
                    for blk in range(NB):
                        pi = acc_bank()
                        for k in range(KD):
                            mm(psum[pi][:, 0:n], aT[:, k, blk * 128:(blk + 1) * 128], sv[:, k, 0:n], k == 0, k == KD - 1, [B_s, B_aT], [PS[pi]])
                        copy_any(outs["vs"][:, 1 + blk, c0 - c_vs0:c0 - c_vs0 + n], psum[pi][:, 0:n], [PS[pi]], [outs["B_vs"]])
                else:
                    pa, pb_ = acc_bank(), acc_bank()
                    for k in range(KD):
                        mm(psum[pa][0:64, :], sv[:, k, 0:64], aT[:, k, :], k == 0, k == KD - 1, [B_s, B_aT], [PS[pa]])
                    for k in range(KD):
                        mm(psum[pb_][0:64, :], sv[:, k, 64:128], aT[:, k, :], k == 0, k == KD - 1, [B_s, B_aT], [PS[pb_]])
                    K.op("dve", "tensor_tensor", dict(out=tmp, in0=psum[pa][0:64, :], in1=cosT, op=ALU.mult), reads=[PS[pa], B_tab], writes=[B_tmp])
                    K.op("dve", "tensor_tensor", dict(out=tmp2_holder[0], in0=psum[pb_][0:64, :], in1=sinT, op=ALU.mult), reads=[PS[pb_], B_tab], writes=[B_tmp])
                    K.op("dve", "tensor_tensor", dict(out=kpeT[0:64, kc0:kc0 + T], in0=tmp, in1=tmp2_holder[0], op=ALU.add), reads=[B_tmp], writes=[B_kpe[kt]])
            return cosT, sinT, B_tab

        def postnorm(y_bf, B_y, sspart, B_ssp, nparts, grow, res_fn, B_res_list, dst_fn, B_dst, o_xs, o_g):
            xs = [sreg(o_xs + i * D * 4, D * 4, F32) for i in range(2)]
            B_xs = G_xs
            g_bc = sreg(o_g, D * 4, F32); B_g = G_g
            K.dma("sp", "dma_start", dict(out=g_bc, in_=g_rows[grow:grow + 1, :].partition_broadcast(128)), writes=[B_g])
            rs = rs_cells; B_rs = Buf("rs")
            for blk in range(NB):
                K.dma("sp", "dma_start", dict(out=xs[0], in_=res_fn(blk)), reads=B_res_list, writes=[B_xs[0]])
                a, b = rs[:, 2 * blk:2 * blk + 1], rs[:, 2 * blk + 1:2 * blk + 2]
                K.op("dve", "tensor_reduce", dict(out=a, in_=sspart[:, blk * nparts:(blk + 1) * nparts], axis=AX.X, op=ALU.add), reads=[B_ssp], writes=[B_rs])
                rstd_from(b, a, D, [B_rs], [B_rs])
                K.op("dve", "scalar_tensor_tensor", dict(out=xs[1], in0=y_bf[:, blk, :], scalar=b, in1=g_bc, op0=ALU.mult, op1=ALU.mult),
                     reads=[B_y, B_rs, B_g], writes=[B_xs[1]])
                K.op("dve", "tensor_tensor", dict(out=xs[0], in0=xs[0], in1=xs[1], op=ALU.add), reads=[B_xs[0], B_xs[1]], writes=[B_xs[0]])
                tk = K.dma("sp", "dma_start", dict(out=dst_fn(blk), in_=xs[0]), reads=[B_xs[0]], writes=[B_dst])
                K.store_tickets.append(tk)

        o_aT = 0
        o_xs = 32 * KB if KD * T * 2 <= 32 * KB else KD * T * 2
        sz_aT = KD * T * 2
        def stage_A1(src_fn, B_src):
            aT = sreg(o_aT, sz_aT, BF16).rearrange("p (k n) -> p k n", k=KD)
            B_aT = Buf("aT")
            norm_transpose(src_fn, B_src, 0, aT, B_aT, sz_aT, sz_aT + 2 * D * 4, sz_aT + 3 * D * 4)
            return aT, B_aT

        o_io = max(sz_aT, KM * T * 2)
        sz_cq = KQ * T * 2
        sz_qs = (SQ // 2) * T * 2
        sz_kd = SKV * (128 + T) * 2
        sz_vs = (1 + NB) * SKV * 64 * 2
        o_cq, o_qs = o_io, o_io + sz_cq
        o_kd = o_qs + sz_qs
        o_vs = o_kd + sz_kd
        o_tmpA = (o_vs + sz_vs + 3) // 4 * 4

        def io_views():
            outs = {}
            outs["cqT"] = sreg(o_cq, sz_cq, BF16).rearrange("p (k n) -> p k n", k=KQ)
            outs["qsT"] = sreg(o_qs, sz_qs, BF16).rearrange("p (k n) -> p k n", k=SQ // 2)
            outs["kdT"] = sreg(o_kd, sz_kd, BF16).rearrange("p (h n) -> p h n", h=SKV)
            outs["vs"] = sreg(o_vs, sz_vs, BF16).rearrange("p (b c) -> p b c", b=1 + NB)
            for n in ("cq", "qs", "kd", "vs"):
                outs["B_" + n] = Buf(n)
            return outs

        B_out = Buf("out_dram")

        STOP = P.get("stop")
        stopn = [P.get("stopn", 0)]
        def stop_if(tag):
            if STOP == tag:
                if stopn[0] == 0:
                    raise StopBuild()
                stopn[0] -= 1
        def build_phases():
            for t in range(NTILE):
                K.barrier()
                stop_if("const")
                aT, B_aT = stage_A1(lambda blk, t=t: x_ctx[t * T + blk * 128:t * T + (blk + 1) * 128, :], None)
                stop_if("a1")
                K.barrier()
                outs = io_views()
                kind = "ctxlast" if t == NTILE - 1 else "ctx"
                inproj(kind, t, aT, B_aT, t, pos_ctx[:, t * T:(t + 1) * T], o_tmpA, outs)
                if kind == "ctxlast":
                    K.op("dve", "tensor_copy", dict(out=kd_prev, in_=outs["kdT"][:, :, T:T + 128]), reads=[outs["B_kd"]], writes=[B_carry])
                    K.op("dve", "tensor_copy", dict(out=v_prev, in_=outs["vs"][:, NB, :]), reads=[outs["B_vs"]], writes=[B_carry])

            stop_if("ctx")
            sc_mla = 1.0 / math.sqrt(192.0)
            sc_swa = 1.0 / 8.0
            slopes = [2.0 ** (-8.0 * (i + 1) / SQ) for i in range(SQ)]
            for t in range(NTILE):
                K.barrier()
                aT, B_aT = stage_A1(lambda blk, t=t: x_own[t * T + blk * 128:t * T + (blk + 1) * 128, :], None)
                K.barrier()
                outs = io_views()
                kt = NTILE + t
                K.op("dve", "tensor_copy", dict(out=outs["kdT"][:, :, 0:128], in_=kd_prev), reads=[B_carry], writes=[outs["B_kd"]])
                K.op("dve", "tensor_copy", dict(out=outs["vs"][:, 0, :], in_=v_prev), reads=[B_carry], writes=[outs["B_vs"]])
                stop_if("carry")
                cosT, sinT, B_tab = inproj("own", t, aT, B_aT, kt, pos_own[:, t * T:(t + 1) * T], o_tmpA, outs)
                K.op("dve", "tensor_copy", dict(out=kd_prev, in_=outs["kdT"][:, :, T:T + 128]), reads=[outs["B_kd"]], writes=[B_carry])
                K.op("dve", "tensor_copy", dict(out=v_prev, in_=outs["vs"][:, NB, :]), reads=[outs["B_vs"]], writes=[B_carry])
                if t == 0:
                    stop_if("inproj0")
                K.barrier()
                mixT = sreg(o_aT, KM * T * 2, BF16).rearrange("p (k n) -> p k n", k=KM)
                B_mix = [Buf("mix%d" % i) for i in range(KM)]
                o1 = o_tmpA
                posq_i = sreg(o1, 512, I32); posq = sreg(o1 + 512, 512, F32)
                posk_i = sreg(o1 + 1024, 8, I32)[:, 0:2]; posk = sreg(o1 + 1032, 8, F32)[:, 0:2]
                dist = [sreg(o1 + 2 * KB + i * 512, 512, F32) for i in range(2)]
                bias4 = sreg(o1 + 3 * KB, 2 * KB, F32).rearrange("p (g n) -> p g n", g=4)
                stmp = sreg(o1 + 5 * KB, 2 * KB, F32)
                pT = [sreg(o1 + 7 * KB + i * KB, KB, BF16) for i in range(2)]
                rden = sreg(o1 + 9 * KB, KB, F32)
                sqB = sreg(o1 + 10 * KB, 512, BF16)
                B_pos, B_dist, B_bias, B_stmp, B_rden, B_sqB = G_pos, Buf("dist"), Buf("bias4"), Buf("stmp"), Buf("rden"), Buf("sqB")
                B_pT = [Buf("pT0"), Buf("pT1")]
                stk = [stmp, bias4.rearrange("p g n -> p (g n)")]
                B_stk = [[Buf("st%d_%d" % (kb, j)) for j in range(4)] for kb in range(2)]
                B_rd = [Buf("rd%d" % j) for j in range(4)]
                STB = 7
                cnt_statB = [0] * NB
                for jq in range(NB):
                    q0 = t * T + jq * 128
                    K.dma("sp", "dma_start", dict(out=posq_i, in_=pos_own[:, q0:q0 + 128].partition_broadcast(128)), writes=[B_pos])
                    if q0 == 0:
                        prev_src = pos_ctx[:, NO - 128:NO]
                    else:
                        prev_src = pos_own[:, q0 - 128:q0]
                    K.dma("sp", "dma_start", dict(out=posk_i[:, 0:1], in_=prev_src.rearrange("o n -> n o")), writes=[B_pos])
                    K.dma("sp", "dma_start", dict(out=posk_i[:, 1:2], in_=pos_own[:, q0:q0 + 128].rearrange("o n -> n o")), writes=[B_pos])
                    K.op("dve", "tensor_copy", dict(out=posq, in_=posq_i), reads=[B_pos], writes=[B_pos])
                    K.op("dve", "tensor_copy", dict(out=posk, in_=posk_i), reads=[B_pos], writes=[B_pos])
                    K.op("dve", "tensor_scalar", dict(out=posk, in0=posk, scalar1=-1.0, scalar2=None, op0=ALU.mult), reads=[B_pos], writes=[B_pos])
                    B_dk = [Buf("dist0"), Buf("dist1")]
                    for kb in range(2):
                        K.op("act", "activation", dict(out=dist[kb], in_=posq, func=AF.Abs, bias=posk[:, kb:kb + 1]),
                             reads=[B_pos], writes=[B_dk[kb]])
                        msk = pmask if kb == 0 else cmask[:, 384:512]
                        K.op("dve", "scalar_tensor_tensor", dict(out=dist[kb], in0=msk, scalar=1.0e6 / NEG, in1=dist[kb], op0=ALU.mult, op1=ALU.add),
                             reads=[B_dk[kb], B_cst], writes=[B_dk[kb]])
                    stop_if("s")
                    for kvh in range(SKV):
                        o_ps, d_ps = 5, 6
                        for kb in range(2):
                            kcol0 = jq * 128 + kb * 128
                            for par in range(2):
                                s_ps = 2 * kb + par
                                pb = par * 64
                                for cl in range(2):
                                    qh = kvh * 4 + cl * 2 + par
                                    c = qh // 2
                                    mm(psum[s_ps][:, cl * 128:(cl + 1) * 128], outs["kdT"][pb:pb + 64, kvh, kcol0:kcol0 + 128],
                                       outs["qsT"][pb:pb + 64, c, jq * 128:(jq + 1) * 128], True, True, [outs["B_kd"], outs["B_qs"]], [PS[s_ps]],
                                       tp=(pb, 0) if pb else None)
                            stop_if("s")
                            for par in range(2):
                                s_ps = 2 * kb + par
                                for cl in range(2):
                                    qh = kvh * 4 + cl * 2 + par
                                    j = par * 2 + cl
                                    K.op("dve", "scalar_tensor_tensor", dict(out=stk[kb][:, j * 128:(j + 1) * 128], in0=dist[kb], scalar=-slopes[qh] / sc_swa,
                                                                             in1=psum[s_ps][:, cl * 128:(cl + 1) * 128], op0=ALU.mult, op1=ALU.add),
                                         reads=[B_dk[kb], PS[s_ps]], writes=[B_stk[kb][j]])
                            stop_if("s")
                            if kb == 0 and q0 == 0:
                                K.op("act", "activation", dict(out=pT[kb], in_=stk[kb], func=AF.Exp, scale=sc_swa, bias=ccol[:, 2:3]), reads=B_stk[kb] + [B_cst], writes=[B_pT[kb]])
                            else:
                                K.op("act", "activation", dict(out=pT[kb], in_=stk[kb], func=AF.Exp, scale=sc_swa), reads=B_stk[kb], writes=[B_pT[kb]])
                            stop_if("s")
                        for g in range(4):
                            par, cl = g % 2, g // 2
                            for kb in range(2):
                                mm(psum[o_ps][par * 64:(par + 1) * 64, cl * 128:(cl + 1) * 128], outs["vs"][:, jq + kb, kvh * 64:(kvh + 1) * 64],
                                   pT[kb][:, (par * 2 + cl) * 128:(par * 2 + cl + 1) * 128], kb == 0, kb == 1, [outs["B_vs"], B_pT[kb]], [PS[o_ps]], tp=(0, par * 64) if par else None)
                            for kb in range(2):
                                mm(psum[d_ps][par * 64:(par + 1) * 64, cl * 128:(cl + 1) * 128], onesb[:, 0:64],
                                   pT[kb][:, (par * 2 + cl) * 128:(par * 2 + cl + 1) * 128], kb == 0, kb == 1, [B_ident, B_pT[kb]], [PS[d_ps]], tp=(0, par * 64) if par else None)
                            stop_if("s")
                        for par in range(2):
                            for cl in range(2):
                                qh = kvh * 4 + cl * 2 + par
                                K.op("act", "activation", dict(out=rden[par * 64:(par + 1) * 64, cl * 128:(cl + 1) * 128],
                                                               in_=psum[d_ps][par * 64:(par + 1) * 64, cl * 128:(cl + 1) * 128],
                                                               func=AF.Identity, bias=esink[par * 64:(par + 1) * 64, qh:qh + 1]),
                                     reads=[PS[d_ps], B_small], writes=[B_rd[par * 2 + cl]])
                        K.op("dve", "reciprocal", dict(out=rden[:, 0:256], in_=rden[:, 0:256]), reads=B_rd, writes=B_rd)
                        stop_if("s")
                        for cl in range(2):
                            ch = H + 2 * kvh + cl
                            K.op("dve", "tensor_tensor", dict(out=mixT[:, ch, jq * 128:(jq + 1) * 128], in0=psum[o_ps][:, cl * 128:(cl + 1) * 128], in1=rden[:, cl * 128:(cl + 1) * 128], op=ALU.mult),
                                 reads=[PS[o_ps]] + B_rd, writes=[B_mix[ch]])
                            K.op("act", "activation", dict(out=sqB[:, cl * 128:(cl + 1) * 128], in_=mixT[:, ch, jq * 128:(jq + 1) * 128], func=AF.Square),
                                 reads=[B_mix[ch]], writes=[B_sqB])
                            mm(psum[STB][:, jq * 128:(jq + 1) * 128], onesb, sqB[:, cl * 128:(cl + 1) * 128], cnt_statB[jq] == 0, cnt_statB[jq] == SQ // 2 - 1, [B_sqB, B_ident], [PS[STB]])
                            cnt_statB[jq] += 1
                            stop_if("s")
                rstdB = sreg(o1 + 11 * KB, 2 * KB, F32); B_rstdB = Buf("rstdB")
                rstd_from(rstdB, psum[STB][:, :], SQ * 64, [PS[STB]], [B_rstdB])
                for c in range(SQ // 2):
                    ch = H + c
                    K.op("dve", "scalar_tensor_tensor", dict(out=mixT[:, ch, :], in0=mixT[:, ch, :], scalar=gcol[:, KQ + KC + ch:KQ + KC + ch + 1], in1=rstdB, op0=ALU.mult, op1=ALU.mult),
                         reads=[B_mix[ch], B_rstdB, B_small], writes=[B_mix[ch]])
                if t == 0:
                    stop_if("swa0")
                K.barrier()
                o2 = o_qs
                nkeys = NO + (t + 1) * T
                nkb = nkeys // 128
                qhT = sreg(o2, 2 * T * 2, BF16).rearrange("p (k n) -> p k n", k=2); B_qh = Buf("qhT")
                khT = sreg(o2 + 2 * KB, NK * 2, BF16); B_kh = Buf("khT")
                vh = sreg(o2 + 2 * KB + NK * 2, NK * 2, BF16).rearrange("p (b d) -> p b d", d=128); B_vh = Buf("vh")
                o3 = o2 + 2 * KB + 2 * NK * 2
                pTm = [sreg(o3 + i * KB, KB, BF16) for i in range(3)]; B_pTm = [Buf("pTm%d" % i) for i in range(3)]
                stm = [sreg(o3 + 3 * KB + i * 2 * KB, 2 * KB, F32) for i in range(2)]; B_stm = [Buf("stm0"), Buf("stm1")]
                rdm = sreg(o3 + 7 * KB, 2 * KB, F32); B_rdm = Buf("rdm")
                t1 = sreg(o3 + 9 * KB, 2 * KB, F32)[0:64, :]; t2 = sreg(o3 + 11 * KB, 2 * KB, F32)[0:64, :]; B_t12 = Buf("t12")
                sqA = sreg(o3 + 13 * KB, KB, BF16); B_sqA = Buf("sqA")
                rstdA = sreg(o3 + 14 * KB, 2 * KB, F32); B_rstdA = Buf("rstdA")
                cosq, sinq, B_cs = cosT, sinT, B_tab
                STA = 7
                pidx = [0]
                K.op("dve", "memset", dict(ap=qhT[64:128, 1, :], constant=0.0), writes=[B_qh])
                for hd in range(H):
                    slot, B_s = w_get(("head", t, hd))
                    wq = slot_view(slot, KQ, 256)
                    wkv = slot[:, KQ * 256:KQ * 256 + KC * 256].rearrange("p (k c) -> p k c", k=KC)
                    for k in range(KQ):
                        mm(psum[4][:, :], wq[:, k, 0:128], outs["cqT"][:, k, :], k == 0, k == KQ - 1, [B_s, outs["B_cq"]], [PS[4]])
                    for k in range(KQ):
                        mm(psum[5][0:64, :], wq[:, k, 128:192], outs["cqT"][:, k, :], k == 0, k == KQ - 1, [B_s, outs["B_cq"]], [PS[5]])
                    for k in range(KQ):
                        mm(psum[6][0:64, :], wq[:, k, 192:256], outs["cqT"][:, k, :], k == 0, k == KQ - 1, [B_s, outs["B_cq"]], [PS[6]])
                    copy_any(qhT[:, 0, :], psum[4][:, :], [PS[4]], [B_qh], eng="act")
                    K.op("dve", "tensor_tensor", dict(out=t1, in0=psum[5][0:64, :], in1=cosq, op=ALU.mult), reads=[PS[5], B_cs], writes=[B_t12])
                    K.op("dve", "tensor_tensor", dict(out=t2, in0=psum[6][0:64, :], in1=sinq, op=ALU.mult), reads=[PS[6], B_cs], writes=[B_t12])
                    K.op("dve", "tensor_tensor", dict(out=qhT[0:64, 1, :], in0=t1, in1=t2, op=ALU.add), reads=[B_t12], writes=[B_qh])
                    for kc in range(nkeys // 512):
                        pi = 4 + kc % 3
                        for k in range(KC):
                            mm(psum[pi][:, :], wkv[:, k, 0:128], ckvT[:, k, kc * 512:(kc + 1) * 512], k == 0, k == KC - 1, [B_s, B_ckv[kc]], [PS[pi]])
                        copy_any(khT[:, kc * 512:(kc + 1) * 512], psum[pi][:, :], [PS[pi]], [B_kh])
                    for kb4 in range(nkb // 4):
                        pi = 4 + kb4 % 3
                        for j in range(4):
                            kb = kb4 * 4 + j
                            for k in range(KC):
                                mm(psum[pi][:, j * 128:(j + 1) * 128], ckvT[:, k, kb * 128:(kb + 1) * 128], wkv[:, k, 128:256], k == 0, k == KC - 1, [B_s, B_ckv[kb // 4]], [PS[pi]])
                        copy_any(vh[:, kb4 * 4:(kb4 + 1) * 4, :], psum[pi][:, :].rearrange("p (b d) -> p b d", d=128), [PS[pi]], [B_vh])
                    O_PS, D_PS = 2, 3
                    def s_mm(kb):
                        s_ps = kb % 2
                        mm(psum[s_ps][:, :], khT[:, kb * 128:(kb + 1) * 128], qhT[:, 0, :], True, False, [B_kh, B_qh], [PS[s_ps]])
                        mm(psum[s_ps][:, :], kpeT[:, kb * 128:(kb + 1) * 128], qhT[:, 1, :], False, True, [B_kpe[kb // 4], B_qh], [PS[s_ps]])
                    s_mm(0)
                    for kb in range(nkb):
                        s_ps = kb % 2
                        if kb + 1 < nkb:
                            s_mm(kb + 1)
                        pp = pidx[0] % 3
                        pidx[0] += 1
                        if kb < NO // 128:
                            K.op("act", "activation", dict(out=pTm[pp], in_=psum[s_ps][:, :], func=AF.Exp, scale=sc_mla, bias=ccol[:, 2:3]),
                                 reads=[PS[s_ps], B_cst], writes=[B_pTm[pp]])
                        elif kb < nkb - 4:
                            K.op("act", "activation", dict(out=pTm[pp], in_=psum[s_ps][:, :], func=AF.Exp, scale=sc_mla),
                                 reads=[PS[s_ps]], writes=[B_pTm[pp]])
                        else:
                            o = (kb - (nkb - 4)) * 128
                            si = kb % 2
                            K.op("dve", "scalar_tensor_tensor", dict(out=stm[si], in0=psum[s_ps][:, :], scalar=sc_mla, in1=cmask[:, 384 - o:384 - o + 512], op0=ALU.mult, op1=ALU.add),
                                 reads=[PS[s_ps], B_cst], writes=[B_stm[si]])
                            K.op("act", "activation", dict(out=pTm[pp], in_=stm[si], func=AF.Exp), reads=[B_stm[si]], writes=[B_pTm[pp]])
                        mm(psum[O_PS][:, :], vh[:, kb, :], pTm[pp], kb == 0, kb == nkb - 1, [B_vh, B_pTm[pp]], [PS[O_PS]])
                        mm(psum[D_PS][:, :], onesb, pTm[pp], kb == 0, kb == nkb - 1, [B_ident, B_pTm[pp]], [PS[D_PS]])
                    K.op("dve", "reciprocal", dict(out=rdm, in_=psum[D_PS][:, :]), reads=[PS[D_PS]], writes=[B_rdm])
                    K.op("dve", "tensor_tensor", dict(out=mixT[:, hd, :], in0=psum[O_PS][:, :], in1=rdm, op=ALU.mult), reads=[PS[O_PS], B_rdm], writes=[B_mix[hd]])
                    K.op("act", "activation", dict(out=sqA, in_=mixT[:, hd, :], func=AF.Square), reads=[B_mix[hd]], writes=[B_sqA])
                    mm(psum[STA][:, :], onesb, sqA, hd == 0, hd == H - 1, [B_sqA, B_ident], [PS[STA]])
                rstd_from(rstdA, psum[STA][:, :], H * 128, [PS[STA]], [B_rstdA])
                for hd in range(H):
                    K.op("dve", "scalar_tensor_tensor", dict(out=mixT[:, hd, :], in0=mixT[:, hd, :], scalar=gcol[:, KQ + KC + hd:KQ + KC + hd + 1], in1=rstdA, op0=ALU.mult, op1=ALU.mult),
                         reads=[B_mix[hd], B_rstdA, B_small], writes=[B_mix[hd]])
                if t == 0:
                    stop_if("mla0")
                K.barrier()
                o_y = KM * T * 2
                y_bf = sreg(o_y, NB * D * 2, BF16).rearrange("p (b d) -> p b d", b=NB); B_y = Buf("y")
                NPW = D // 256
                sspart = ssp_cells; B_ssp = Buf("ssp")
                junk = sreg(o_y + NB * D * 2, KB, F32)[:, 0:256]; B_junk = Buf("junk")
                for cb in range(NPW):
                    slot, B_s = w_get(("wo", t, cb))
                    sv = slot_view(slot, KM, 256)
                    for blk in range(NB):
                        pi = (cb * NB + blk) % 4
                        for k in range(KM):
                            mm(psum[pi][:, 0:256], mixT[:, k, blk * 128:(blk + 1) * 128], sv[:, k, :], k == 0, k == KM - 1, [B_s, B_mix[k]], [PS[pi]])
                        copy_any(y_bf[:, blk, cb * 256:(cb + 1) * 256], psum[pi][:, 0:256], [PS[pi]], [B_y])
                        K.op("act", "activation", dict(out=junk, in_=y_bf[:, blk, cb * 256:(cb + 1) * 256], func=AF.Square, accum_out=sspart[:, blk * NPW + cb:blk * NPW + cb + 1]),
                             reads=[B_y], writes=[B_junk, B_ssp])
                K.barrier()
                postnorm(y_bf, B_y, sspart, B_ssp, NPW, 1,
                         lambda blk, t=t: x_own[t * T + blk * 128:t * T + (blk + 1) * 128, :], [],
                         lambda blk, t=t: out[t * T + blk * 128:t * T + (blk + 1) * 128, :], B_out, 0, o_y + NB * D * 2 + KB)

            stop_if("h")
            K.barrier()
            assert KD * T * 2 <= sz_ckv + NK * 2 or True
            for t in range(NTILE):
                K.barrier()
                fT = region(o_ckv, KD * T * 2, BF16).rearrange("p (k n) -> p k n", k=KD); B_fT = Buf("fT")
                norm_transpose(lambda blk, t=t: out[t * T + blk * 128:t * T + (blk + 1) * 128, :], B_out, 2, fT, B_fT, 0, 2 * D * 4, 3 * D * 4)
                K.barrier()
                actT = sreg(0, KF * T * 2, BF16).rearrange("p (k n) -> p k n", k=KF)
                B_act = [Buf("act%d" % j) for j in range(KF)]
                sgt = [sreg(KF * T * 2 + i * 2 * KB, 2 * KB, F32) for i in range(2)]
                B_sg = [Buf("sg0"), Buf("sg1")]
                for j in range(KF):
                    slot, B_s = w_get(("gu", t, j))
                    wg = slot[:, 0:KD * 128].rearrange("p (k c) -> p k c", k=KD)
                    wu = slot[:, KD * 128:2 * KD * 128].rearrange("p (k c) -> p k c", k=KD)
                    pg, pu = (j % 2) * 2, (j % 2) * 2 + 1
                    for k in range(KD):
                        mm(psum[pg][:, :], wg[:, k, :], fT[:, k, :], k == 0, k == KD - 1, [B_s, B_fT], [PS[pg]])
                    for k in range(KD):
                        mm(psum[pu][:, :], wu[:, k, :], fT[:, k, :], k == 0, k == KD - 1, [B_s, B_fT], [PS[pu]])
                    K.op("act", "activation", dict(out=sgt[j % 2], in_=psum[pg][:, :], func=AF.Silu), reads=[PS[pg]], writes=[B_sg[j % 2]])
                    K.op("dve", "tensor_tensor", dict(out=actT[:, j, :], in0=psum[pu][:, :], in1=sgt[j % 2], op=ALU.mult), reads=[PS[pu], B_sg[j % 2]], writes=[B_act[j]])
                K.barrier()
                y_bf = region(o_ckv, NB * D * 2, BF16).rearrange("p (b d) -> p b d", b=NB); B_y = Buf("y2")
                NPD = D // 512
                sspart = ssp_cells; B_ssp = Buf("ssp2")
                junk = region(o_cs, T * 2, BF16); B_junk = Buf("junk2")
                KG = SLOT // 2 // 512
                for cb in range(NPD):
                    banks = [4 + b for b in range(NB)] if cb % 2 == 0 else [b for b in range(NB)]
                    for k0 in range(0, KF, KG):
                        kn = min(KG, KF - k0)
                        slot, B_s = w_get(("dn", t, cb, k0))
                        sv = slot_view(slot, kn, 512)
                        for kk in range(kn):
                            k = k0 + kk
                            for blk in range(NB):
                                mm(psum[banks[blk]][:, :], actT[:, k, blk * 128:(blk + 1) * 128], sv[:, kk, :], k == 0, k == KF - 1, [B_s, B_act[k]], [PS[banks[blk]]])
                    for blk in range(NB):
                        pi = banks[blk]
                        copy_any(y_bf[:, blk, cb * 512:(cb + 1) * 512], psum[pi][:, :], [PS[pi]], [B_y])
                        K.op("act", "activation", dict(out=junk, in_=y_bf[:, blk, cb * 512:(cb + 1) * 512], func=AF.Square, accum_out=sspart[:, blk * NPD + cb:blk * NPD + cb + 1]),
                             reads=[B_y], writes=[B_junk, B_ssp])
                K.barrier()
                postnorm(y_bf, B_y, sspart, B_ssp, NPD, 3,
                         lambda blk, t=t: out[t * T + blk * 128:t * T + (blk + 1) * 128, :], [B_out],
                         lambda blk, t=t: out[t * T + blk * 128:t * T + (blk + 1) * 128, :], B_out, 0, 2 * D * 4)

        try:
            build_phases()
            assert wstate["cur"] == len(wplan), (wstate["cur"], len(wplan))
        except StopBuild:
            print("STOPPED at", STOP)
            for b in B_slot:
                for tk in b.w.values():
                    K.engs["sp"].wait(*tk)

        e = K.engs["sp"]
        for sem, val in K.store_tickets:
            e.wait(sem, val)
        n_ins = {n: len(e.q) for n, e in K.engs.items()}
        print("instruction counts", n_ins)
        with nc.Block() as block:
            K.emit(block)
    return nc


def host_prep(inp, P):
    D, NO, H, SQ, SKV, QR, KVR = P["D"], P["NO"], P["H"], P["SQ"], P["SKV"], P["QR"], P["KVR"]
    KQ, KC, KM = P["KQ"], P["KC"], P["KM"]
    f32 = np.float32
    x = np.asarray(inp["x"], f32)
    pos = np.asarray(inp["positions"]).astype(np.int32)
    w_in = np.asarray(inp["w_in"], f32)[0]
    sp = np.cumsum([QR, KVR, 64, SQ * 64, SKV * 64, SKV * 64])
    cq, ckv, kr, qs, ks, vs = np.split(w_in, sp[:-1], axis=1)
    kdup = ks.reshape(D, SKV, 1, 64).repeat(2, axis=2).reshape(D, SKV * 128)
    kr_sw = np.concatenate([kr[:, 32:], kr[:, :32]], axis=1)
    w_in2 = np.concatenate([cq, ckv, qs, kdup, vs, kr, kr_sw], axis=1)
    assert w_in2.shape[1] == P["CIN"]
    KD, KF, DFF = P["KD"], P["KF"], P["DFF"]
    w3 = w_in2.reshape(KD, 128, P["CIN"])
    blks = []
    for (c0, n) in P["_in_blocks"]:
        blks.append(np.ascontiguousarray(w3[:, :, c0:c0 + n].transpose(1, 0, 2)).reshape(128, KD * n))
    w_in_r = np.ascontiguousarray(np.concatenate(blks, axis=1))
    wuq = np.asarray(inp["w_uq"], f32)[0].reshape(QR, H, 192)
    wuq2 = np.concatenate([wuq, wuq[:, :, 160:192], wuq[:, :, 128:160]], axis=2)
    wuq_r = np.ascontiguousarray(wuq2.reshape(KQ, 128, H, 256).transpose(1, 2, 0, 3)).reshape(128, H * KQ * 256)
    wukv = np.asarray(inp["w_ukv"], f32)[0]
    wukv_r = np.ascontiguousarray(wukv.reshape(KC, 128, H, 256).transpose(1, 2, 0, 3)).reshape(128, H * KC * 256)
    w_o = np.asarray(inp["w_o"], f32)[0]
    w_o_r = np.ascontiguousarray(w_o.reshape(KM, 128, D // 256, 256).transpose(1, 2, 0, 3)).reshape(128, -1)
    w_gate_r = np.ascontiguousarray(np.asarray(inp["w_gate"], f32)[0].reshape(KD, 128, KF, 128).transpose(1, 2, 0, 3)).reshape(128, -1)
    w_up_r = np.ascontiguousarray(np.asarray(inp["w_up"], f32)[0].reshape(KD, 128, KF, 128).transpose(1, 2, 0, 3)).reshape(128, -1)
    w_down_r = np.ascontiguousarray(np.asarray(inp["w_down"], f32)[0].reshape(KF, 128, D // 512, 512).transpose(1, 2, 0, 3)).reshape(128, -1)
    g_rows = np.ascontiguousarray(np.stack([np.asarray(inp[k], f32)[0] for k in ("attn_pre_g", "attn_post_g", "ffn_pre_g", "ffn_post_g")]))
    col = lambda v, n: np.asarray(v, f32)[0].reshape(n, 128).T
    g_cols = np.ascontiguousarray(np.concatenate([col(inp["q_norm_g"], KQ), col(inp["kv_norm_g"], KC),
                                                  col(inp["grp_a_g"], H), col(inp["grp_b_g"], SQ // 2)], axis=1))
    sinks = np.ascontiguousarray(np.asarray(inp["swa_sinks"], f32).reshape(1, SQ))
    cst = np.zeros((2, 128, 128 + 896 + 128 + 8), f32)
    kk = np.arange(128)[:, None]
    cst[:, :, 0:128] = np.eye(128, dtype=f32)
    cc = np.arange(896)[None, :]
    cst[:, :, 128:1024] = np.where(cc - 384 >= kk, 0.0, NEG)
    qq = np.arange(128)[None, :]
    cst[:, :, 1024:1152] = np.where(qq < kk, 0.0, NEG)
    inv = (1.0 / (10000.0 ** (np.arange(0, 64, 2, dtype=np.float32) / 64.0))).astype(f32)
    cst[:, 0:64, 1152] = np.concatenate([inv, inv])
    cst[:, 0:64, 1153] = np.concatenate([np.full(32, math.pi), np.full(32, 2 * math.pi)])
    cst[0, :, 1154] = NEG
    cst[1, :, 1154] = 0.0
    cst[:, :, 1155] = -math.pi
    cst[:, :, 1156] = EPS
    cst[:, :, 1157] = 2.5 * math.pi
    maps = []
    for core in range(P["NCORE"]):
        b, hf = core // 2, core % 2
        maps.append({
            "x_own": np.ascontiguousarray(x[b, hf * NO:(hf + 1) * NO]),
            "x_ctx": np.ascontiguousarray(x[b, 0:NO]),
            "pos_own": np.ascontiguousarray(pos[b, hf * NO:(hf + 1) * NO].reshape(1, NO)),
            "pos_ctx": np.ascontiguousarray(pos[b, 0:NO].reshape(1, NO)),
            "w_in": w_in_r, "w_uq": wuq_r, "w_ukv": wukv_r, "w_o": w_o_r, "w_gate": w_gate_r, "w_up": w_up_r, "w_down": w_down_r,
            "g_rows": g_rows, "g_cols": g_cols, "sinks": sinks, "cst": np.ascontiguousarray(cst[hf]),
        })
    return maps


def run(inp, P, trace=False):
    P = derive(P)
    nc = build(P)
    maps = host_prep(inp, P)
    res = run_bass_kernel_spmd(nc, maps, core_ids=list(range(P["NCORE"])), **({"trace": True} if trace else {}))
    NO = P["NO"]
    outp = np.zeros((P["B"], P["SEQ"], P["D"]), np.float32)
    for core in range(P["NCORE"]):
        b, hf = core // 2, core % 2
        outp[b, hf * NO:(hf + 1) * NO] = res.results[core]["out"]
    return outp, res


def kernel(**inputs):
    outp, _ = run(inputs, FULL)
    return outp
```

```python
import math
from contextlib import ExitStack
import numpy as np
import concourse.bass as bass
import concourse.mybir as mybir
from concourse.bass_utils import run_bass_kernel_spmd

F32 = mybir.dt.float32
BF16 = mybir.dt.bfloat16
I32 = mybir.dt.int32
AF = mybir.ActivationFunctionType
ALU = mybir.AluOpType
AX = mybir.AxisListType

NEG = -30000.0
EPS = 1e-6
SAME_ENGINE_WAITS = True

FULL = dict(D=4096, B=4, SEQ=4096, H=16, QR=1024, KVR=512, SQ=32, SKV=8, DFF=11008, T=512)


def derive(P):
    P = dict(P)
    P["NO"] = P["SEQ"] // 2
    P["KD"] = P["D"] // 128
    P["KQ"] = P["QR"] // 128
    P["KC"] = P["KVR"] // 128
    P["KF"] = P["DFF"] // 128
    P["MIXA"] = P["H"] * 128
    P["MIXB"] = P["SQ"] * 64
    P["KM"] = (P["MIXA"] + P["MIXB"]) // 128
    P["NTILE"] = P["NO"] // P["T"]
    P["CIN"] = P["QR"] + P["KVR"] + P["SQ"] * 64 + P["SKV"] * 128 + P["SKV"] * 64 + 128
    P["NCORE"] = 2 * P["B"]
    blks = []
    def rng(c0, n):
        for i in range(0, n, 256):
            blks.append((c0 + i, min(256, n - i)))
    c = 0
    for n in (P["QR"], P["KVR"], P["SQ"] * 64, P["SKV"] * 128, P["SKV"] * 64, 128):
        rng(c, n)
        c += n
    P["_in_blocks"] = blks
    return P


class Eng:
    def __init__(self, name, sem):
        self.name, self.sem = name, sem
        self.q = []
        self.cnt = 0
        self.seen = {}

    def wait(self, sem, val):
        if val <= 0:
            return
        if sem is self.sem and (not SAME_ENGINE_WAITS or self.name in ("pe", "sp", "pool")):
            return
        if self.seen.get(id(sem), 0) < val:
            self.q.append(("wait", sem, val))
            self.seen[id(sem)] = val


class Buf:
    def __init__(self, name):
        self.name = name
        self.w = {}
        self.r = {}
        self.dsem = None
        self.dcnt = 0
        self.excl = name.startswith("ps") and name[2:].isdigit()


class StopBuild(Exception):
    pass


class Ker:
    def __init__(self, nc, stack):
        self.nc = nc
        self.stack = stack
        self.engs = {}
        for n in ("pe", "act", "dve", "pool", "sp"):
            self.engs[n] = Eng(n, stack.enter_context(nc.semaphore("s_" + n)))
        self.store_tickets = []

    def newsem(self, name):
        return self.stack.enter_context(self.nc.semaphore(name))

    def _deps(self, e, reads, writes):
        need = {}
        def add(t):
            sem, val = t
            k = id(sem)
            if k not in need or need[k][1] < val:
                need[k] = (sem, val)
        for b in reads:
            for t in b.w.values():
                add(t)
            if b.excl:
                for t in b.r.values():
                    add(t)
        for b in writes:
            for t in b.w.values():
                add(t)
            for t in b.r.values():
                add(t)
        for sem, val in need.values():
            e.wait(sem, val)

    def op(self, en, meth, kw, reads=(), writes=()):
        fn = (meth, kw)
        e = self.engs[en]
        self._deps(e, reads, writes)
        e.cnt += 1
        t = (e.sem, e.cnt)
        for b in reads:
            b.r[en] = t
        for b in writes:
            b.w[en] = t
            b.r = {}
        e.q.append(("op", fn, e.sem, 1))

    def dma(self, en, meth, kw, reads=(), writes=(), sembuf=None):
        fn = (meth, kw)
        e = self.engs[en]
        self._deps(e, reads, writes)
        sb = sembuf if sembuf is not None else (writes[0] if writes else reads[0])
        if sb.dsem is None:
            sb.dsem = self.newsem("d_" + sb.name)
        sb.dcnt += 1
        t = (sb.dsem, 16 * sb.dcnt)
        for b in reads:
            b.r["dma_" + sb.name] = t
        for b in writes:
            b.w["dma_" + sb.name] = t
            b.r = {}
        e.q.append(("op", fn, sb.dsem, 16))
        return t

    def barrier(self, names=("pe", "act", "dve", "sp")):
        snap = [(self.engs[n].sem, self.engs[n].cnt) for n in names]
        for n in names:
            e = self.engs[n]
            for sem, val in snap:
                if sem is not e.sem:
                    e.wait(sem, val)
            for sem, val in self.store_tickets:
                e.wait(sem, val)

    def emit(self, block):
        nc = self.nc
        hmap = {"pe": "tensor", "act": "scalar", "dve": "vector", "pool": "gpsimd", "sp": "sync"}
        for n, e in self.engs.items():
            def body(h, e=e):
                for it in e.q:
                    if it[0] == "wait":
                        h.wait_ge(it[1], it[2])
                    else:
                        getattr(h, it[1][0])(**it[1][1]).then_inc(it[2], it[3])
            getattr(block, hmap[n])(body)


def build(P):
    D, NO, T, KD, KQ, KC, KF, KM = P["D"], P["NO"], P["T"], P["KD"], P["KQ"], P["KC"], P["KF"], P["KM"]
    H, SQ, SKV, QR, KVR, DFF, CIN = P["H"], P["SQ"], P["SKV"], P["QR"], P["KVR"], P["DFF"], P["CIN"]
    NTILE = P["NTILE"]
    NB = T // 128
    NK = 2 * NO
    KMAX = max(KD, KM)
    assert T == 512 and KF * 128 == DFF

    nc = bass.Bass("TRN2", target_bir_lowering=False)
    dt_in = lambda n, s, d=F32: nc.dram_tensor(n, s, d, kind="ExternalInput").ap()
    x_own = dt_in("x_own", [NO, D])
    x_ctx = dt_in("x_ctx", [NO, D])
    pos_own = dt_in("pos_own", [1, NO], I32)
    pos_ctx = dt_in("pos_ctx", [1, NO], I32)
    w_in = dt_in("w_in", [128, KD * CIN])
    w_uq = dt_in("w_uq", [128, H * KQ * 256])
    w_ukv = dt_in("w_ukv", [128, H * KC * 256])
    w_o = dt_in("w_o", [128, (D // 256) * KM * 256])
    w_gate = dt_in("w_gate", [128, KF * KD * 128])
    w_up = dt_in("w_up", [128, KF * KD * 128])
    w_down = dt_in("w_down", [128, (D // 512) * KF * 512])
    g_rows = dt_in("g_rows", [4, D])
    NCOL = KQ + KC + KM
    g_cols = dt_in("g_cols", [128, NCOL])
    sinks = dt_in("sinks", [1, SQ])
    NCST = 128 + 896 + 128 + 8
    cst = dt_in("cst", [128, NCST])
    out = nc.dram_tensor("out", [NO, D], F32, kind="ExternalOutput").ap()

    stack = ExitStack()
    with stack:
        K = Ker(nc, stack)
        ARENA = 206 * 1024
        arena = stack.enter_context(nc.sbuf_tensor("arena", [128, ARENA // 2], BF16))
        psum = [stack.enter_context(nc.psum_tensor("ps%d" % i, [128, 512], F32)) for i in range(8)]
        PS = [Buf("ps%d" % i) for i in range(8)]

        def region(off, nbytes, dtype, shape=None):
            assert off % 4 == 0 and off + nbytes <= ARENA, (off, nbytes)
            a = arena[:, off // 2:(off + nbytes) // 2]
            if dtype is F32 or dtype is I32:
                a = a.bitcast(dtype)
            return a

        KB = 1024
        o_cst = 0
        o_ckv = 5 * KB
        sz_ckv = KC * NK * 2
        o_kpe = o_ckv + sz_ckv
        o_slots = o_kpe + NK * 2
        SLOT = KMAX * 256 * 2
        NSLOT = 3
        o_carry = o_slots + NSLOT * SLOT
        o_small = o_carry + SKV * 128 * 2 + SKV * 64 * 2
        o_small = (o_small + 3) // 4 * 4
        o_cs = o_small + 2 * KB
        o_stage = o_cs + 2 * T * 4
        STAGE = ARENA - o_stage
        print("arena: stage offset", o_stage, "stage bytes", STAGE)

        cst_t = region(o_cst, NCST * 4, F32)
        ident_f = cst_t[:, 0:128]
        cmask = cst_t[:, 128:128 + 896]
        pmask = cst_t[:, 1024:1152]
        ccol = cst_t[:, 1152:1160]
        B_cst = Buf("cst")
        small = region(o_small, 2 * KB, F32)
        B_small = Buf("small")
        gcol = small[:, 0:NCOL]
        esink = small[:, 64:64 + SQ]
        small_next = [64 + SQ]
        def cell(n=1):
            a = small[:, small_next[0]:small_next[0] + n]
            small_next[0] += n
            assert small_next[0] <= 512 - 128
            return a
        ss_cells, rs_cells, ssp_cells = cell(2 * NB), cell(2 * NB), cell(NB * (D // 256))
        G_xs = [Buf("gxs0"), Buf("gxs1")]
        G_g, G_pos, G_tmp = Buf("gg"), Buf("gpos"), Buf("gtmp")
        identb = region(o_small + 2 * KB - 512, 256, BF16)
        onesb = region(o_small + 2 * KB - 256, 256, BF16)
        B_ident = Buf("ident")

        ckvT = region(o_ckv, sz_ckv, BF16).rearrange("p (k n) -> p k n", k=KC)
        B_ckv = [Buf("ckv%d" % i) for i in range(NK // 512)]
        kpeT = region(o_kpe, NK * 2, BF16)
        B_kpe = [Buf("kpe%d" % i) for i in range(NK // 512)]
        slots = [region(o_slots + i * SLOT, SLOT, BF16) for i in range(NSLOT)]
        B_slot = [Buf("slot%d" % i) for i in range(NSLOT)]
        kd_prev = region(o_carry, SKV * 128 * 2, BF16).rearrange("p (h n) -> p h n", h=SKV)
        v_prev = region(o_carry + SKV * 128 * 2, SKV * 64 * 2, BF16)
        B_carry = Buf("carry")

        def sreg(off, nbytes, dtype):
            assert off + nbytes <= STAGE, ("stage overflow", off, nbytes, STAGE)
            return region(o_stage + off, nbytes, dtype)

        wplan = []
        wstate = {"issued": 0, "cur": 0}

        def w_issue_upto(i):
            while wstate["issued"] <= min(i, len(wplan) - 1):
                j = wstate["issued"]
                key, parts = wplan[j]
                s = j % NSLOT
                for (dstf, src) in parts:
                    dst = dstf(slots[s])
                    K.dma("pool", "dma_start", dict(out=dst, in_=src),
                          writes=[B_slot[s]])
                wstate["issued"] += 1

        def w_get(key):
            j = wstate["cur"]
            assert wplan[j][0] == key, (wplan[j][0], key)
            w_issue_upto(j + NSLOT - 1)
            wstate["cur"] += 1
            return slots[j % NSLOT], B_slot[j % NSLOT]

        def slot_view(slot, kk, cols):
            return slot[:, 0:kk * cols].rearrange("p (k c) -> p k c", k=kk)

        def chunked(ap, n):
            b_ = 2048
            while n % b_:
                b_ //= 2
            return ap.rearrange("p (a b) -> p a b", b=b_)

        c_q0 = 0
        c_kv0 = QR
        c_qs0 = QR + KVR
        c_kd0 = c_qs0 + SQ * 64
        c_vs0 = c_kd0 + SKV * 128
        c_kr0 = c_vs0 + SKV * 64
        def inproj_blocks(kind):
            blks = []
            def rng(name, c0, n):
                for i in range(0, n, 256):
                    blks.append((name, i // 128, c0 + i, min(256, n - i)))
            if kind == "own":
                rng("cq", c_q0, QR)
            rng("ckv", c_kv0, KVR)
            if kind in ("own", "ctxlast"):
                if kind == "own":
                    rng("qs", c_qs0, SQ * 64)
                rng("kd", c_kd0, SKV * 128)
                rng("vs", c_vs0, SKV * 64)
            rng("kr", c_kr0, 128)
            return blks

        def plan():
            def flat(dram, off, n, soff=0):
                return (lambda s, n=n, soff=soff: chunked(s[:, soff:soff + n], n), chunked(dram[:, off:off + n], n))
            in_off = {}
            o = 0
            for (name, mc, c0, n) in inproj_blocks("own"):
                in_off[(name, mc)] = o
                o += KD * n
            assert o == KD * CIN
            for t in range(NTILE):
                kind = "ctxlast" if t == NTILE - 1 else "ctx"
                for (name, mc, c0, n) in inproj_blocks(kind):
                    wplan.append((("in", "c", t, name, mc), [flat(w_in, in_off[(name, mc)], KD * n)]))
            for t in range(NTILE):
                for (name, mc, c0, n) in inproj_blocks("own"):
                    wplan.append((("in", "o", t, name, mc), [flat(w_in, in_off[(name, mc)], KD * n)]))
                for h in range(H):
                    wplan.append((("head", t, h), [flat(w_uq, h * KQ * 256, KQ * 256),
                                                   flat(w_ukv, h * KC * 256, KC * 256, KQ * 256)]))
                for cb in range(D // 256):
                    wplan.append((("wo", t, cb), [flat(w_o, cb * KM * 256, KM * 256)]))
            for t in range(NTILE):
                for j in range(KF):
                    wplan.append((("gu", t, j), [flat(w_gate, j * KD * 128, KD * 128),
                                                 flat(w_up, j * KD * 128, KD * 128, KD * 128)]))
                KG = SLOT // 2 // 512
                for cb in range(D // 512):
                    for k0 in range(0, KF, KG):
                        kn = min(KG, KF - k0)
                        wplan.append((("dn", t, cb, k0), [flat(w_down, (cb * KF + k0) * 512, kn * 512)]))
        plan()
        assert KQ * 256 + KC * 256 <= SLOT // 2

        def mm(out_ap, lhsT, rhs, start, stop, reads, writes, tp=None):
            kw = {}
            if tp is not None:
                kw["tile_position"] = tp
            K.op("pe", "matmul", dict(out=out_ap, lhsT=lhsT, rhs=rhs, start=start, stop=stop, **kw), reads=reads, writes=writes)

        evac_rr = [0]
        def copy_any(out_ap, in_ap, reads, writes, eng=None):
            if eng is None:
                eng = ("act", "dve")[evac_rr[0] % 2]
                evac_rr[0] += 1
            if eng == "act":
                K.op("act", "activation", dict(out=out_ap, in_=in_ap, func=AF.Copy), reads=reads, writes=writes)
            else:
                K.op(eng, "tensor_copy", dict(out=out_ap, in_=in_ap), reads=reads, writes=writes)

        def rstd_from(dst, src, n_feat, reads, writes):
            K.op("act", "activation", dict(out=dst, in_=src, func=AF.Sqrt, scale=1.0 / n_feat, bias=ccol[:dst.shape[0], 4:5]),
                 reads=list(reads) + [B_cst], writes=writes)
            K.op("dve", "reciprocal", dict(out=dst, in_=dst), reads=writes, writes=writes)

        K.dma("sp", "dma_start", dict(out=cst_t, in_=cst), writes=[B_cst])
        K.dma("sp", "dma_start", dict(out=gcol, in_=g_cols), writes=[B_small])
        K.dma("sp", "dma_start", dict(out=esink, in_=sinks.partition_broadcast(128)), writes=[B_small])
        K.op("act", "activation", dict(out=esink, in_=esink, func=AF.Exp), reads=[B_small], writes=[B_small])
        K.op("dve", "tensor_copy", dict(out=identb, in_=ident_f), reads=[B_cst], writes=[B_ident])
        K.op("dve", "memset", dict(ap=onesb, constant=1.0), writes=[B_ident])
        K.op("dve", "memset", dict(ap=kpeT[64:128, :], constant=0.0), writes=B_kpe)

        ps_rr = {"n": 0}

        def norm_transpose(src_rows_fn, B_src, grow, aT, B_aT, o_xs, o_g, o_abf):
            xs = [sreg(o_xs + i * D * 4, D * 4, F32) for i in range(2)]
            B_xs = G_xs
            g_bc = sreg(o_g, D * 4, F32)
            B_g = G_g
            abf = [sreg(o_abf + i * D * 2, D * 2, BF16) for i in range(2)]
            B_abf = [Buf("abf0"), Buf("abf1")]
            ss = ss_cells
            B_ss = Buf("ss")
            K.dma("sp", "dma_start", dict(out=g_bc, in_=g_rows[grow:grow + 1, :].partition_broadcast(128)), writes=[B_g])
            for blk in range(NB):
                x_t, B_x = xs[blk % 2], B_xs[blk % 2]
                a_t, B_a = abf[blk % 2], B_abf[blk % 2]
                src = src_rows_fn(blk)
                K.dma("sp", "dma_start", dict(out=x_t, in_=src), reads=[B_src] if B_src else [], writes=[B_x])
                s1 = ss[:, 2 * blk:2 * blk + 1]
                s2 = ss[:, 2 * blk + 1:2 * blk + 2]
                B_ssb = Buf("ss%d" % blk)
                K.op("act", "activation", dict(out=aT[:, :, blk * 128:(blk + 1) * 128], in_=x_t.rearrange("p (k n) -> p k n", k=KD), func=AF.Square, accum_out=s1),
                     reads=[B_x], writes=[B_ssb])
                rstd_from(s2, s1, D, [B_ssb], [B_ssb])
                K.op("dve", "scalar_tensor_tensor", dict(out=a_t, in0=x_t, scalar=s2, in1=g_bc, op0=ALU.mult, op1=ALU.mult),
                     reads=[B_x, B_ssb, B_g], writes=[B_a])
                for c0 in range(0, KD, 8):
                    cn = min(8, KD - c0)
                    pi = ps_rr["n"] % 2
                    ps_rr["n"] += 1
                    pb = psum[pi][:, :].bitcast(BF16)
                    for c in range(cn):
                        K.op("pe", "transpose", dict(out=pb[:, c * 128:(c + 1) * 128], in_=a_t[:, (c0 + c) * 128:(c0 + c + 1) * 128], identity=identb),
                             reads=[B_a, B_ident], writes=[PS[pi]])
                    copy_any(aT[:, c0:c0 + cn, blk * 128:(blk + 1) * 128], pb[:, 0:cn * 128].rearrange("p (c n) -> p c n", c=cn),
                             reads=[PS[pi]], writes=[B_aT])

        def rope_tables(pos_ap, cosT, sinT, tmp, tmpi, B_tab, B_tmp):
            K.dma("sp", "dma_start", dict(out=tmpi, in_=pos_ap.partition_broadcast(64)), writes=[B_tmp])
            K.op("dve", "tensor_copy", dict(out=tmp, in_=tmpi), reads=[B_tmp], writes=[B_tmp])
            for (dst, shcol) in ((cosT, ccol[0:64, 5:6]), (sinT, ccol[0:64, 1:2])):
                K.op("dve", "tensor_scalar", dict(out=dst, in0=tmp, scalar1=ccol[0:64, 0:1], scalar2=shcol, op0=ALU.mult, op1=ALU.add),
                     reads=[B_tmp, B_cst], writes=[B_tab])
                K.op("dve", "tensor_scalar", dict(out=tmpi, in0=dst, scalar1=1.0 / (2 * math.pi), scalar2=None, op0=ALU.mult),
                     reads=[B_tab], writes=[B_tmp])
                K.op("dve", "tensor_copy", dict(out=tmp2_holder[0], in_=tmpi), reads=[B_tmp], writes=[B_tmp])
                K.op("dve", "scalar_tensor_tensor", dict(out=dst, in0=tmp2_holder[0], scalar=-2 * math.pi, in1=dst, op0=ALU.mult, op1=ALU.add),
                     reads=[B_tmp, B_tab], writes=[B_tab])
                K.op("dve", "tensor_scalar", dict(out=tmp2_holder[0], in0=dst, scalar1=math.pi, scalar2=-2 * math.pi, op0=ALU.is_gt, op1=ALU.mult),
                     reads=[B_tab], writes=[B_tmp])
                K.op("dve", "tensor_tensor", dict(out=dst, in0=dst, in1=tmp2_holder[0], op=ALU.add), reads=[B_tab, B_tmp], writes=[B_tab])
                K.op("dve", "tensor_scalar", dict(out=tmp2_holder[0], in0=dst, scalar1=-math.pi, scalar2=2 * math.pi, op0=ALU.is_lt, op1=ALU.mult),
                     reads=[B_tab], writes=[B_tmp])
                K.op("dve", "tensor_tensor", dict(out=dst, in0=dst, in1=tmp2_holder[0], op=ALU.add), reads=[B_tab, B_tmp], writes=[B_tab])
                K.op("dve", "tensor_scalar", dict(out=dst, in0=dst, scalar1=-3.1415925, scalar2=3.1415925, op0=ALU.max, op1=ALU.min),
                     reads=[B_tab], writes=[B_tab])
                K.op("act", "activation", dict(out=dst, in_=dst, func=AF.Sin), reads=[B_tab], writes=[B_tab])
        tmp2_holder = [None]

        def inproj(kind, t, aT, B_aT, kt, pos_ap, o0, outs):
            sq = sreg(o0, T * 2, BF16); B_sq = Buf("sq")
            rstd = sreg(o0 + 1 * KB, T * 4, F32); B_rstd = Buf("rstd")
            cosT = region(o_cs, T * 4, F32)[0:64, :]
            sinT = region(o_cs + T * 4, T * 4, F32)[0:64, :]
            tmp = sreg(o0 + 3 * KB, T * 4, F32)[0:64, :]
            tmpi = sreg(o0 + 5 * KB, T * 4, I32)[0:64, :]
            tmp2_holder[0] = sreg(o0 + 7 * KB, T * 4, F32)[0:64, :]
            B_tab, B_tmp = Buf("tab"), G_tmp
            rope_tables(pos_ap, cosT, sinT, tmp, tmpi, B_tab, B_tmp)
            stop_if("rope")
            kc0 = kt * T
            stat = {}
            acc_rr = [0]
            def acc_bank():
                i = 2 + acc_rr[0] % 3
                acc_rr[0] += 1
                return i
            nchunks = {"cq": KQ, "ckv": KC}
            done = {"cq": 0, "ckv": 0}
            ck = "c" if kind != "own" else "o"
            for (name, mc, c0, n) in inproj_blocks(kind):
                slot, B_s = w_get(("in", ck, t, name, mc))
                sv = slot_view(slot, KD, n)
                stop_if("blk")
                if STOP == "w1":
                    K.op("dve", "tensor_copy", dict(out=kpeT[:, 0:64], in_=sv[:, 0, 0:64]), reads=[B_s], writes=[B_kpe[0]])
                    stop_if("w1")
                if name in ("cq", "ckv", "qs", "kd"):
                    for m in range(n // 128):
                        pi = acc_bank()
                        for k in range(KD):
                            mm(psum[pi][:, :], sv[:, k, m * 128:(m + 1) * 128], aT[:, k, :], k == 0, k == KD - 1,
                               [B_s, B_aT], [PS[pi]])
                        ch = mc + m
                        if name in ("cq", "ckv"):
                            if name == "cq":
                                dst, B_d, sp_i = outs["cqT"][:, ch, :], outs["B_cq"], 6
                            else:
                                dst, B_d, sp_i = ckvT[:, ch, kc0:kc0 + T], B_ckv[kt], 6
                            K.op("act", "activation", dict(out=dst, in_=psum[pi][:, :], func=AF.Copy), reads=[PS[pi]], writes=[B_d])
                            K.op("dve", "tensor_tensor", dict(out=sq, in0=dst, in1=dst, op=ALU.mult), reads=[B_d], writes=[B_sq])
                            mm(psum[sp_i][:, :], onesb, sq, done[name] == 0, done[name] == nchunks[name] - 1, [B_sq, B_ident], [PS[sp_i]])
                            done[name] += 1
                            if done[name] == nchunks[name]:
                                nf = QR if name == "cq" else KVR
                                goff = 0 if name == "cq" else KQ
                                rstd_from(rstd, psum[sp_i][:, :], nf, [PS[sp_i]], [B_rstd])
                                for c in range(nchunks[name]):
                                    d2 = outs["cqT"][:, c, :] if name == "cq" else ckvT[:, c, kc0:kc0 + T]
                                    K.op("dve", "scalar_tensor_tensor", dict(out=d2, in0=d2, scalar=gcol[:, goff + c:goff + c + 1], in1=rstd, op0=ALU.mult, op1=ALU.mult),
                                         reads=[B_d, B_rstd, B_small], writes=[B_d])
                        elif name == "qs":
                            copy_any(outs["qsT"][:, ch, :], psum[pi][:, :], [PS[pi]], [outs["B_qs"]])
                        else:
                            copy_any(outs["kdT"][:, ch, 128:128 + T], psum[pi][:, :], [PS[pi]], [outs["B_kd"]])
                elif name == "vs":
                    for blk in range(NB):
                        pi = acc_bank()
                        for k in range(KD):
                            mm(psum[pi][:, 0:n], aT[:, k, blk * 128:(blk + 1) * 128], sv[:, k, 0:n], k == 0, k == KD - 1, [B_s, B_aT], [PS[pi]])
                        copy_any(outs["vs"][:, 1 + blk, c0 - c_vs0:c0 - c_vs0 + n], psum[pi][:, 0:n], [PS[pi]], [outs["B_vs"]])
                else:
                    pa, pb_ = acc_bank(), acc_bank()
                    for k in range(KD):
                        mm(psum[pa][0:64, :], sv[:, k, 0:64], aT[:, k, :], k == 0, k == KD - 1, [B_s, B_aT], [PS[pa]])
                    for k in range(KD):
                        mm(psum[pb_][0:64, :], sv[:, k, 64:128], aT[:, k, :], k == 0, k == KD - 1, [B_s, B_aT], [PS[pb_]])
                    K.op("dve", "tensor_tensor", dict(out=tmp, in0=psum[pa][0:64, :], in1=cosT, op=ALU.mult), reads=[PS[pa], B_tab], writes=[B_tmp])
                    K.op("dve", "tensor_tensor", dict(out=tmp2_holder[0], in0=psum[pb_][0:64, :], in1=sinT, op=ALU.mult), reads=[PS[pb_], B_tab], writes=[B_tmp])
                    K.op("dve", "tensor_tensor", dict(out=kpeT[0:64, kc0:kc0 + T], in0=tmp, in1=tmp2_holder[0], op=ALU.add), reads=[B_tmp], writes=[B_kpe[kt]])
            return cosT, sinT, B_tab

        def postnorm(y_bf, B_y, sspart, B_ssp, nparts, grow, res_fn, B_res_list, dst_fn, B_dst, o_xs, o_g):
            xs = [sreg(o_xs + i * D * 4, D * 4, F32) for i in range(2)]
            B_xs = G_xs
            g_bc = sreg(o_g, D * 4, F32); B_g = G_g
            K.dma("sp", "dma_start", dict(out=g_bc, in_=g_rows[grow:grow + 1, :].partition_broadcast(128)), writes=[B_g])
            rs = rs_cells; B_rs = Buf("rs")
            for blk in range(NB):
                K.dma("sp", "dma_start", dict(out=xs[0], in_=res_fn(blk)), reads=B_res_list, writes=[B_xs[0]])
                a, b = rs[:, 2 * blk:2 * blk + 1], rs[:, 2 * blk + 1:2 * blk + 2]
                K.op("dve", "tensor_reduce", dict(out=a, in_=sspart[:, blk * nparts:(blk + 1) * nparts], axis=AX.X, op=ALU.add), reads=[B_ssp], writes=[B_rs])
                rstd_from(b, a, D, [B_rs], [B_rs])
                K.op("dve", "scalar_tensor_tensor", dict(out=xs[1], in0=y_bf[:, blk, :], scalar=b, in1=g_bc, op0=ALU.mult, op1=ALU.mult),
                     reads=[B_y, B_rs, B_g], writes=[B_xs[1]])
                K.op("dve", "tensor_tensor", dict(out=xs[0], in0=xs[0], in1=xs[1], op=ALU.add), reads=[B_xs[0], B_xs[1]], writes=[B_xs[0]])
                tk = K.dma("sp", "dma_start", dict(out=dst_fn(blk), in_=xs[0]), reads=[B_xs[0]], writes=[B_dst])
                K.store_tickets.append(tk)

        o_aT = 0
        o_xs = 32 * KB if KD * T * 2 <= 32 * KB else KD * T * 2
        sz_aT = KD * T * 2
        def stage_A1(src_fn, B_src):
            aT = sreg(o_aT, sz_aT, BF16).rearrange("p (k n) -> p k n", k=KD)
            B_aT = Buf("aT")
            norm_transpose(src_fn, B_src, 0, aT, B_aT, sz_aT, sz_aT + 2 * D * 4, sz_aT + 3 * D * 4)
            return aT, B_aT

        o_io = max(sz_aT, KM * T * 2)
        sz_cq = KQ * T * 2
        sz_qs = (SQ // 2) * T * 2
        sz_kd = SKV * (128 + T) * 2
        sz_vs = (1 + NB) * SKV * 64 * 2
        o_cq, o_qs = o_io, o_io + sz_cq
        o_kd = o_qs + sz_qs
        o_vs = o_kd + sz_kd
        o_tmpA = (o_vs + sz_vs + 3) // 4 * 4

        def io_views():
            outs = {}
            outs["cqT"] = sreg(o_cq, sz_cq, BF16).rearrange("p (k n) -> p k n", k=KQ)
            outs["qsT"] = sreg(o_qs, sz_qs, BF16).rearrange("p (k n) -> p k n", k=SQ // 2)
            outs["kdT"] = sreg(o_kd, sz_kd, BF16).rearrange("p (h n) -> p h n", h=SKV)
            outs["vs"] = sreg(o_vs, sz_vs, BF16).rearrange("p (b c) -> p b c", b=1 + NB)
            for n in ("cq", "qs", "kd", "vs"):
                outs["B_" + n] = Buf(n)
            return outs

        B_out = Buf("out_dram")

        STOP = P.get("stop")
        stopn = [P.get("stopn", 0)]
        def stop_if(tag):
            if STOP == tag:
                if stopn[0] == 0:
                    raise StopBuild()
                stopn[0] -= 1
        def build_phases():
            for t in range(NTILE):
                K.barrier()
                stop_if("const")
                aT, B_aT = stage_A1(lambda blk, t=t: x_ctx[t * T + blk * 128:t * T + (blk + 1) * 128, :], None)
                stop_if("a1")
                K.barrier()
                outs = io_views()
                kind = "ctxlast" if t == NTILE - 1 else "ctx"
                inproj(kind, t, aT, B_aT, t, pos_ctx[:, t * T:(t + 1) * T], o_tmpA, outs)
                if kind == "ctxlast":
                    K.op("dve", "tensor_copy", dict(out=kd_prev, in_=outs["kdT"][:, :, T:T + 128]), reads=[outs["B_kd"]], writes=[B_carry])
                    K.op("dve", "tensor_copy", dict(out=v_prev, in_=outs["vs"][:, NB, :]), reads=[outs["B_vs"]], writes=[B_carry])

            stop_if("ctx")
            sc_mla = 1.0 / math.sqrt(192.0)
            sc_swa = 1.0 / 8.0
            slopes = [2.0 ** (-8.0 * (i + 1) / SQ) for i in range(SQ)]
            for t in range(NTILE):
                K.barrier()
                aT, B_aT = stage_A1(lambda blk, t=t: x_own[t * T + blk * 128:t * T + (blk + 1) * 128, :], None)
                K.barrier()
                outs = io_views()
                kt = NTILE + t
                K.op("dve", "tensor_copy", dict(out=outs["kdT"][:, :, 0:128], in_=kd_prev), reads=[B_carry], writes=[outs["B_kd"]])
                K.op("dve", "tensor_copy", dict(out=outs["vs"][:, 0, :], in_=v_prev), reads=[B_carry], writes=[outs["B_vs"]])
                stop_if("carry")
                cosT, sinT, B_tab = inproj("own", t, aT, B_aT, kt, pos_own[:, t * T:(t + 1) * T], o_tmpA, outs)
                K.op("dve", "tensor_copy", dict(out=kd_prev, in_=outs["kdT"][:, :, T:T + 128]), reads=[outs["B_kd"]], writes=[B_carry])
                K.op("dve", "tensor_copy", dict(out=v_prev, in_=outs["vs"][:, NB, :]), reads=[outs["B_vs"]], writes=[B_carry])
                if t == 0:
                    stop_if("inproj0")
                K.barrier()
                mixT = sreg(o_aT, KM * T * 2, BF16).rearrange("p (k n) -> p k n", k=KM)
                B_mix = [Buf("mix%d" % i) for i in range(KM)]
                o1 = o_tmpA
                posq_i = sreg(o1, 512, I32); posq = sreg(o1 + 512, 512, F32)
                posk_i = sreg(o1 + 1024, 8, I32)[:, 0:2]; posk = sreg(o1 + 1032, 8, F32)[:, 0:2]
                dist = [sreg(o1 + 2 * KB + i * 512, 512, F32) for i in range(2)]
                bias4 = sreg(o1 + 3 * KB, 2 * KB, F32).rearrange("p (g n) -> p g n", g=4)
                stmp = sreg(o1 + 5 * KB, 2 * KB, F32)
                pT = [sreg(o1 + 7 * KB + i * KB, KB, BF16) for i in range(2)]
                rden = sreg(o1 + 9 * KB, KB, F32)
                sqB = sreg(o1 + 10 * KB, 512, BF16)
                B_pos, B_dist, B_bias, B_stmp, B_rden, B_sqB = G_pos, Buf("dist"), Buf("bias4"), Buf("stmp"), Buf("rden"), Buf("sqB")
                B_pT = [Buf("pT0"), Buf("pT1")]
                stk = [stmp, bias4.rearrange("p g n -> p (g n)")]
                B_stk = [[Buf("st%d_%d" % (kb, j)) for j in range(4)] for kb in range(2)]
                B_rd = [Buf("rd%d" % j) for j in range(4)]
                STB = 7
                cnt_statB = [0] * NB
                for jq in range(NB):
                    q0 = t * T + jq * 128
                    K.dma("sp", "dma_start", dict(out=posq_i, in_=pos_own[:, q0:q0 + 128].partition_broadcast(128)), writes=[B_pos])
                    if q0 == 0:
                        prev_src = pos_ctx[:, NO - 128:NO]
                    else:
                        prev_src = pos_own[:, q0 - 128:q0]
                    K.dma("sp", "dma_start", dict(out=posk_i[:, 0:1], in_=prev_src.rearrange("o n -> n o")), writes=[B_pos])
                    K.dma("sp", "dma_start", dict(out=posk_i[:, 1:2], in_=pos_own[:, q0:q0 + 128].rearrange("o n -> n o")), writes=[B_pos])
                    K.op("dve", "tensor_copy", dict(out=posq, in_=posq_i), reads=[B_pos], writes=[B_pos])
                    K.op("dve", "tensor_copy", dict(out=posk, in_=posk_i), reads=[B_pos], writes=[B_pos])
                    K.op("dve", "tensor_scalar", dict(out=posk, in0=posk, scalar1=-1.0, scalar2=None, op0=ALU.mult), reads=[B_pos], writes=[B_pos])
                    B_dk = [Buf("dist0"), Buf("dist1")]
                    for kb in range(2):
                        K.op("act", "activation", dict(out=dist[kb], in_=posq, func=AF.Abs, bias=posk[:, kb:kb + 1]),
                             reads=[B_pos], writes=[B_dk[kb]])
                        msk = pmask if kb == 0 else cmask[:, 384:512]
                        K.op("dve", "scalar_tensor_tensor", dict(out=dist[kb], in0=msk, scalar=1.0e6 / NEG, in1=dist[kb], op0=ALU.mult, op1=ALU.add),
                             reads=[B_dk[kb], B_cst], writes=[B_dk[kb]])
                    stop_if("s")
                    for kvh in range(SKV):
                        o_ps, d_ps = 5, 6
                        for kb in range(2):
                            kcol0 = jq * 128 + kb * 128
                            for par in range(2):
                                s_ps = 2 * kb + par
                                pb = par * 64
                                for cl in range(2):
                                    qh = kvh * 4 + cl * 2 + par
                                    c = qh // 2
                                    mm(psum[s_ps][:, cl * 128:(cl + 1) * 128], outs["kdT"][pb:pb + 64, kvh, kcol0:kcol0 + 128],
                                       outs["qsT"][pb:pb + 64, c, jq * 128:(jq + 1) * 128], True, True, [outs["B_kd"], outs["B_qs"]], [PS[s_ps]],
                                       tp=(pb, 0) if pb else None)
                            stop_if("s")
                            for par in range(2):
                                s_ps = 2 * kb + par
                                for cl in range(2):
                                    qh = kvh * 4 + cl * 2 + par
                                    j = par * 2 + cl
                                    K.op("dve", "scalar_tensor_tensor", dict(out=stk[kb][:, j * 128:(j + 1) * 128], in0=dist[kb], scalar=-slopes[qh] / sc_swa,
                                                                             in1=psum[s_ps][:, cl * 128:(cl + 1) * 128], op0=ALU.mult, op1=ALU.add),
                                         reads=[B_dk[kb], PS[s_ps]], writes=[B_stk[kb][j]])
                            stop_if("s")
                            if kb == 0 and q0 == 0:
                                K.op("act", "activation", dict(out=pT[kb], in_=stk[kb], func=AF.Exp, scale=sc_swa, bias=ccol[:, 2:3]), reads=B_stk[kb] + [B_cst], writes=[B_pT[kb]])
                            else:
                                K.op("act", "activation", dict(out=pT[kb], in_=stk[kb], func=AF.Exp, scale=sc_swa), reads=B_stk[kb], writes=[B_pT[kb]])
                            stop_if("s")
                        for g in range(4):
                            par, cl = g % 2, g // 2
                            for kb in range(2):
                                mm(psum[o_ps][par * 64:(par + 1) * 64, cl * 128:(cl + 1) * 128], outs["vs"][:, jq + kb, kvh * 64:(kvh + 1) * 64],
                                   pT[kb][:, (par * 2 + cl) * 128:(par * 2 + cl + 1) * 128], kb == 0, kb == 1, [outs["B_vs"], B_pT[kb]], [PS[o_ps]], tp=(0, par * 64) if par else None)
                            for kb in range(2):
                                mm(psum[d_ps][par * 64:(par + 1) * 64, cl * 128:(cl + 1) * 128], onesb[:, 0:64],
                                   pT[kb][:, (par * 2 + cl) * 128:(par * 2 + cl + 1) * 128], kb == 0, kb == 1, [B_ident, B_pT[kb]], [PS[d_ps]], tp=(0, par * 64) if par else None)
                            stop_if("s")
                        for par in range(2):
                            for cl in range(2):
                                qh = kvh * 4 + cl * 2 + par
                                K.op("act", "activation", dict(out=rden[par * 64:(par + 1) * 64, cl * 128:(cl + 1) * 128],
                                                               in_=psum[d_ps][par * 64:(par + 1) * 64, cl * 128:(cl + 1) * 128],
                                                               func=AF.Identity, bias=esink[par * 64:(par + 1) * 64, qh:qh + 1]),
                                     reads=[PS[d_ps], B_small], writes=[B_rd[par * 2 + cl]])
                        K.op("dve", "reciprocal", dict(out=rden[:, 0:256], in_=rden[:, 0:256]), reads=B_rd, writes=B_rd)
                        stop_if("s")
                        for cl in range(2):
                            ch = H + 2 * kvh + cl
                            K.op("dve", "tensor_tensor", dict(out=mixT[:, ch, jq * 128:(jq + 1) * 128], in0=psum[o_ps][:, cl * 128:(cl + 1) * 128], in1=rden[:, cl * 128:(cl + 1) * 128], op=ALU.mult),
                                 reads=[PS[o_ps]] + B_rd, writes=[B_mix[ch]])
                            K.op("act", "activation", dict(out=sqB[:, cl * 128:(cl + 1) * 128], in_=mixT[:, ch, jq * 128:(jq + 1) * 128], func=AF.Square),
                                 reads=[B_mix[ch]], writes=[B_sqB])
                            mm(psum[STB][:, jq * 128:(jq + 1) * 128], onesb, sqB[:, cl * 128:(cl + 1) * 128], cnt_statB[jq] == 0, cnt_statB[jq] == SQ // 2 - 1, [B_sqB, B_ident], [PS[STB]])
                            cnt_statB[jq] += 1
                            stop_if("s")
                rstdB = sreg(o1 + 11 * KB, 2 * KB, F32); B_rstdB = Buf("rstdB")
                rstd_from(rstdB, psum[STB][:, :], SQ * 64, [PS[STB]], [B_rstdB])
                for c in range(SQ // 2):
                    ch = H + c
                    K.op("dve", "scalar_tensor_tensor", dict(out=mixT[:, ch, :], in0=mixT[:, ch, :], scalar=gcol[:, KQ + KC + ch:KQ + KC + ch + 1], in1=rstdB, op0=ALU.mult, op1=ALU.mult),
                         reads=[B_mix[ch], B_rstdB, B_small], writes=[B_mix[ch]])
                if t == 0:
                    stop_if("swa0")
                K.barrier()
                o2 = o_qs
                nkeys = NO + (t + 1) * T
                nkb = nkeys // 128
                qhT = sreg(o2, 2 * T * 2, BF16).rearrange("p (k n) -> p k n", k=2); B_qh = Buf("qhT")
                khT = sreg(o2 + 2 * KB, NK * 2, BF16); B_kh = Buf("khT")
                vh = sreg(o2 + 2 * KB + NK * 2, NK * 2, BF16).rearrange("p (b d) -> p b d", d=128); B_vh = Buf("vh")
                o3 = o2 + 2 * KB + 2 * NK * 2
                pTm = [sreg(o3 + i * KB, KB, BF16) for i in range(3)]; B_pTm = [Buf("pTm%d" % i) for i in range(3)]
                stm = [sreg(o3 + 3 * KB + i * 2 * KB, 2 * KB, F32) for i in range(2)]; B_stm = [Buf("stm0"), Buf("stm1")]
                rdm = sreg(o3 + 7 * KB, 2 * KB, F32); B_rdm = Buf("rdm")
                t1 = sreg(o3 + 9 * KB, 2 * KB, F32)[0:64, :]; t2 = sreg(o3 + 11 * KB, 2 * KB, F32)[0:64, :]; B_t12 = Buf("t12")
                sqA = sreg(o3 + 13 * KB, KB, BF16); B_sqA = Buf("sqA")
                rstdA = sreg(o3 + 14 * KB, 2 * KB, F32); B_rstdA = Buf("rstdA")
                cosq, sinq, B_cs = cosT, sinT, B_tab
                STA = 7
                pidx = [0]
                K.op("dve", "memset", dict(ap=qhT[64:128, 1, :], constant=0.0), writes=[B_qh])
                for hd in range(H):
                    slot, B_s = w_get(("head", t, hd))
                    wq = slot_view(slot, KQ, 256)
                    wkv = slot[:, KQ * 256:KQ * 256 + KC * 256].rearrange("p (k c) -> p k c", k=KC)
                    for k in range(KQ):
                        mm(psum[4][:, :], wq[:, k, 0:128], outs["cqT"][:, k, :], k == 0, k == KQ - 1, [B_s, outs["B_cq"]], [PS[4]])
                    for k in range(KQ):
                        mm(psum[5][0:64, :], wq[:, k, 128:192], outs["cqT"][:, k, :], k == 0, k == KQ - 1, [B_s, outs["B_cq"]], [PS[5]])
                    for k in range(KQ):
                        mm(psum[6][0:64, :], wq[:, k, 192:256], outs["cqT"][:, k, :], k == 0, k == KQ - 1, [B_s, outs["B_cq"]], [PS[6]])
                    copy_any(qhT[:, 0, :], psum[4][:, :], [PS[4]], [B_qh], eng="act")
                    K.op("dve", "tensor_tensor", dict(out=t1, in0=psum[5][0:64, :], in1=cosq, op=ALU.mult), reads=[PS[5], B_cs], writes=[B_t12])
                    K.op("dve", "tensor_tensor", dict(out=t2, in0=psum[6][0:64, :], in1=sinq, op=ALU.mult), reads=[PS[6], B_cs], writes=[B_t12])
                    K.op("dve", "tensor_tensor", dict(out=qhT[0:64, 1, :], in0=t1, in1=t2, op=ALU.add), reads=[B_t12], writes=[B_qh])
                    for kc in range(nkeys // 512):
                        pi = 4 + kc % 3
                        for k in range(KC):
                            mm(psum[pi][:, :], wkv[:, k, 0:128], ckvT[:, k, kc * 512:(kc + 1) * 512], k == 0, k == KC - 1, [B_s, B_ckv[kc]], [PS[pi]])
                        copy_any(khT[:, kc * 512:(kc + 1) * 512], psum[pi][:, :], [PS[pi]], [B_kh])
                    for kb4 in range(nkb // 4):
                        pi = 4 + kb4 % 3
                        for j in range(4):
                            kb = kb4 * 4 + j
                            for k in range(KC):
                                mm(psum[pi][:, j * 128:(j + 1) * 128], ckvT[:, k, kb * 128:(kb + 1) * 128], wkv[:, k, 128:256], k == 0, k == KC - 1, [B_s, B_ckv[kb // 4]], [PS[pi]])
                        copy_any(vh[:, kb4 * 4:(kb4 + 1) * 4, :], psum[pi][:, :].rearrange("p (b d) -> p b d", d=128), [PS[pi]], [B_vh])
                    O_PS, D_PS = 2, 3
                    def s_mm(kb):
                        s_ps = kb % 2
                        mm(psum[s_ps][:, :], khT[:, kb * 128:(kb + 1) * 128], qhT[:, 0, :], True, False, [B_kh, B_qh], [PS[s_ps]])
                        mm(psum[s_ps][:, :], kpeT[:, kb * 128:(kb + 1) * 128], qhT[:, 1, :], False, True, [B_kpe[kb // 4], B_qh], [PS[s_ps]])
                    s_mm(0)
                    for kb in range(nkb):
                        s_ps = kb % 2
                        if kb + 1 < nkb:
                            s_mm(kb + 1)
                        pp = pidx[0] % 3
                        pidx[0] += 1
                        if kb < NO // 128:
                            K.op("act", "activation", dict(out=pTm[pp], in_=psum[s_ps][:, :], func=AF.Exp, scale=sc_mla, bias=ccol[:, 2:3]),
                                 reads=[PS[s_ps], B_cst], writes=[B_pTm[pp]])
                        elif kb < nkb - 4:
                            K.op("act", "activation", dict(out=pTm[pp], in_=psum[s_ps][:, :], func=AF.Exp, scale=sc_mla),
                                 reads=[PS[s_ps]], writes=[B_pTm[pp]])
                        else:
                            o = (kb - (nkb - 4)) * 128
                            si = kb % 2
                            K.op("dve", "scalar_tensor_tensor", dict(out=stm[si], in0=psum[s_ps][:, :], scalar=sc_mla, in1=cmask[:, 384 - o:384 - o + 512], op0=ALU.mult, op1=ALU.add),
                                 reads=[PS[s_ps], B_cst], writes=[B_stm[si]])
                            K.op("act", "activation", dict(out=pTm[pp], in_=stm[si], func=AF.Exp), reads=[B_stm[si]], writes=[B_pTm[pp]])
                        mm(psum[O_PS][:, :], vh[:, kb, :], pTm[pp], kb == 0, kb == nkb - 1, [B_vh, B_pTm[pp]], [PS[O_PS]])
                        mm(psum[D_PS][:, :], onesb, pTm[pp], kb == 0, kb == nkb - 1, [B_ident, B_pTm[pp]], [PS[D_PS]])
                    K.op("dve", "reciprocal", dict(out=rdm, in_=psum[D_PS][:, :]), reads=[PS[D_PS]], writes=[B_rdm])
                    K.op("dve", "tensor_tensor", dict(out=mixT[:, hd, :], in0=psum[O_PS][:, :], in1=rdm, op=ALU.mult), reads=[PS[O_PS], B_rdm], writes=[B_mix[hd]])
                    K.op("act", "activation", dict(out=sqA, in_=mixT[:, hd, :], func=AF.Square), reads=[B_mix[hd]], writes=[B_sqA])
                    mm(psum[STA][:, :], onesb, sqA, hd == 0, hd == H - 1, [B_sqA, B_ident], [PS[STA]])
                rstd_from(rstdA, psum[STA][:, :], H * 128, [PS[STA]], [B_rstdA])
                for hd in range(H):
                    K.op("dve", "scalar_tensor_tensor", dict(out=mixT[:, hd, :], in0=mixT[:, hd, :], scalar=gcol[:, KQ + KC + hd:KQ + KC + hd + 1], in1=rstdA, op0=ALU.mult, op1=ALU.mult),
                         reads=[B_mix[hd], B_rstdA, B_small], writes=[B_mix[hd]])
                if t == 0:
                    stop_if("mla0")
                K.barrier()
                o_y = KM * T * 2
                y_bf = sreg(o_y, NB * D * 2, BF16).rearrange("p (b d) -> p b d", b=NB); B_y = Buf("y")
                NPW = D // 256
                sspart = ssp_cells; B_ssp = Buf("ssp")
                junk = sreg(o_y + NB * D * 2, KB, F32)[:, 0:256]; B_junk = Buf("junk")
                for cb in range(NPW):
                    slot, B_s = w_get(("wo", t, cb))
                    sv = slot_view(slot, KM, 256)
                    for blk in range(NB):
                        pi = (cb * NB + blk) % 4
                        for k in range(KM):
                            mm(psum[pi][:, 0:256], mixT[:, k, blk * 128:(blk + 1) * 128], sv[:, k, :], k == 0, k == KM - 1, [B_s, B_mix[k]], [PS[pi]])
                        copy_any(y_bf[:, blk, cb * 256:(cb + 1) * 256], psum[pi][:, 0:256], [PS[pi]], [B_y])
                        K.op("act", "activation", dict(out=junk, in_=y_bf[:, blk, cb * 256:(cb + 1) * 256], func=AF.Square, accum_out=sspart[:, blk * NPW + cb:blk * NPW + cb + 1]),
                             reads=[B_y], writes=[B_junk, B_ssp])
                K.barrier()
                postnorm(y_bf, B_y, sspart, B_ssp, NPW, 1,
                         lambda blk, t=t: x_own[t * T + blk * 128:t * T + (blk + 1) * 128, :], [],
                         lambda blk, t=t: out[t * T + blk * 128:t * T + (blk + 1) * 128, :], B_out, 0, o_y + NB * D * 2 + KB)

            stop_if("h")
            K.barrier()
            assert KD * T * 2 <= sz_ckv + NK * 2 or True
            for t in range(NTILE):
                K.barrier()
                fT = region(o_ckv, KD * T * 2, BF16).rearrange("p (k n) -> p k n", k=KD); B_fT = Buf("fT")
                norm_transpose(lambda blk, t=t: out[t * T + blk * 128:t * T + (blk + 1) * 128, :], B_out, 2, fT, B_fT, 0, 2 * D * 4, 3 * D * 4)
                K.barrier()
                actT = sreg(0, KF * T * 2, BF16).rearrange("p (k n) -> p k n", k=KF)
                B_act = [Buf("act%d" % j) for j in range(KF)]
                sgt = [sreg(KF * T * 2 + i * 2 * KB, 2 * KB, F32) for i in range(2)]
                B_sg = [Buf("sg0"), Buf("sg1")]
                for j in range(KF):
                    slot, B_s = w_get(("gu", t, j))
                    wg = slot[:, 0:KD * 128].rearrange("p (k c) -> p k c", k=KD)
                    wu = slot[:, KD * 128:2 * KD * 128].rearrange("p (k c) -> p k c", k=KD)
                    pg, pu = (j % 2) * 2, (j % 2) * 2 + 1
                    for k in range(KD):
                        mm(psum[pg][:, :], wg[:, k, :], fT[:, k, :], k == 0, k == KD - 1, [B_s, B_fT], [PS[pg]])
                    for k in range(KD):
                        mm(psum[pu][:, :], wu[:, k, :], fT[:, k, :], k == 0, k == KD - 1, [B_s, B_fT], [PS[pu]])
                    K.op("act", "activation", dict(out=sgt[j % 2], in_=psum[pg][:, :], func=AF.Silu), reads=[PS[pg]], writes=[B_sg[j % 2]])
                    K.op("dve", "tensor_tensor", dict(out=actT[:, j, :], in0=psum[pu][:, :], in1=sgt[j % 2], op=ALU.mult), reads=[PS[pu], B_sg[j % 2]], writes=[B_act[j]])
                K.barrier()
                y_bf = region(o_ckv, NB * D * 2, BF16).rearrange("p (b d) -> p b d", b=NB); B_y = Buf("y2")
                NPD = D // 512
                sspart = ssp_cells; B_ssp = Buf("ssp2")
                junk = region(o_cs, T * 2, BF16); B_junk = Buf("junk2")
                KG = SLOT // 2 // 512
                for cb in range(NPD):
                    banks = [4 + b for b in range(NB)] if cb % 2 == 0 else [b for b in range(NB)]
                    for k0 in range(0, KF, KG):
                        kn = min(KG, KF - k0)
                        slot, B_s = w_get(("dn", t, cb, k0))
                        sv = slot_view(slot, kn, 512)
                        for kk in range(kn):
                            k = k0 + kk
                            for blk in range(NB):
                                mm(psum[banks[blk]][:, :], actT[:, k, blk * 128:(blk + 1) * 128], sv[:, kk, :], k == 0, k == KF - 1, [B_s, B_act[k]], [PS[banks[blk]]])
                    for blk in range(NB):
                        pi = banks[blk]
                        copy_any(y_bf[:, blk, cb * 512:(cb + 1) * 512], psum[pi][:, :], [PS[pi]], [B_y])
                        K.op("act", "activation", dict(out=junk, in_=y_bf[:, blk, cb * 512:(cb + 1) * 512], func=AF.Square, accum_out=sspart[:, blk * NPD + cb:blk * NPD + cb + 1]),
                             reads=[B_y], writes=[B_junk, B_ssp])
                K.barrier()
                postnorm(y_bf, B_y, sspart, B_ssp, NPD, 3,
                         lambda blk, t=t: out[t * T + blk * 128:t * T + (blk + 1) * 128, :], [B_out],
                         lambda blk, t=t: out[t * T + blk * 128:t * T + (blk + 1) * 128, :], B_out, 0, 2 * D * 4)

        try:
            build_phases()
            assert wstate["cur"] == len(wplan), (wstate["cur"], len(wplan))
        except StopBuild:
            print("STOPPED at", STOP)
            for b in B_slot:
                for tk in b.w.values():
                    K.engs["sp"].wait(*tk)

        e = K.engs["sp"]
        for sem, val in K.store_tickets:
            e.wait(sem, val)
        n_ins = {n: len(e.q) for n, e in K.engs.items()}
        print("instruction counts", n_ins)
        with nc.Block() as block:
            K.emit(block)
    return nc


def host_prep(inp, P):
    D, NO, H, SQ, SKV, QR, KVR = P["D"], P["NO"], P["H"], P["SQ"], P["SKV"], P["QR"], P["KVR"]
    KQ, KC, KM = P["KQ"], P["KC"], P["KM"]
    f32 = np.float32
    x = np.asarray(inp["x"], f32)
    pos = np.asarray(inp["positions"]).astype(np.int32)
    w_in = np.asarray(inp["w_in"], f32)[0]
    sp = np.cumsum([QR, KVR, 64, SQ * 64, SKV * 64, SKV * 64])
    cq, ckv, kr, qs, ks, vs = np.split(w_in, sp[:-1], axis=1)
    kdup = ks.reshape(D, SKV, 1, 64).repeat(2, axis=2).reshape(D, SKV * 128)
    kr_sw = np.concatenate([kr[:, 32:], kr[:, :32]], axis=1)
    w_in2 = np.concatenate([cq, ckv, qs, kdup, vs, kr, kr_sw], axis=1)
    assert w_in2.shape[1] == P["CIN"]
    KD, KF, DFF = P["KD"], P["KF"], P["DFF"]
    w3 = w_in2.reshape(KD, 128, P["CIN"])
    blks = []
    for (c0, n) in P["_in_blocks"]:
        blks.append(np.ascontiguousarray(w3[:, :, c0:c0 + n].transpose(1, 0, 2)).reshape(128, KD * n))
    w_in_r = np.ascontiguousarray(np.concatenate(blks, axis=1))
    wuq = np.asarray(inp["w_uq"], f32)[0].reshape(QR, H, 192)
    wuq2 = np.concatenate([wuq, wuq[:, :, 160:192], wuq[:, :, 128:160]], axis=2)
    wuq_r = np.ascontiguousarray(wuq2.reshape(KQ, 128, H, 256).transpose(1, 2, 0, 3)).reshape(128, H * KQ * 256)
    wukv = np.asarray(inp["w_ukv"], f32)[0]
    wukv_r = np.ascontiguousarray(wukv.reshape(KC, 128, H, 256).transpose(1, 2, 0, 3)).reshape(128, H * KC * 256)
    w_o = np.asarray(inp["w_o"], f32)[0]
    w_o_r = np.ascontiguousarray(w_o.reshape(KM, 128, D // 256, 256).transpose(1, 2, 0, 3)).reshape(128, -1)
    w_gate_r = np.ascontiguousarray(np.asarray(inp["w_gate"], f32)[0].reshape(KD, 128, KF, 128).transpose(1, 2, 0, 3)).reshape(128, -1)
    w_up_r = np.ascontiguousarray(np.asarray(inp["w_up"], f32)[0].reshape(KD, 128, KF, 128).transpose(1, 2, 0, 3)).reshape(128, -1)
    w_down_r = np.ascontiguousarray(np.asarray(inp["w_down"], f32)[0].reshape(KF, 128, D // 512, 512).transpose(1, 2, 0, 3)).reshape(128, -1)
    g_rows = np.ascontiguousarray(np.stack([np.asarray(inp[k], f32)[0] for k in ("attn_pre_g", "attn_post_g", "ffn_pre_g", "ffn_post_g")]))
    col = lambda v, n: np.asarray(v, f32)[0].reshape(n, 128).T
    g_cols = np.ascontiguousarray(np.concatenate([col(inp["q_norm_g"], KQ), col(inp["kv_norm_g"], KC),
                                                  col(inp["grp_a_g"], H), col(inp["grp_b_g"], SQ // 2)], axis=1))
    sinks = np.ascontiguousarray(np.asarray(inp["swa_sinks"], f32).reshape(1, SQ))
    cst = np.zeros((2, 128, 128 + 896 + 128 + 8), f32)
    kk = np.arange(128)[:, None]
    cst[:, :, 0:128] = np.eye(128, dtype=f32)
    cc = np.arange(896)[None, :]
    cst[:, :, 128:1024] = np.where(cc - 384 >= kk, 0.0, NEG)
    qq = np.arange(128)[None, :]
    cst[:, :, 1024:1152] = np.where(qq < kk, 0.0, NEG)
    inv = (1.0 / (10000.0 ** (np.arange(0, 64, 2, dtype=np.float32) / 64.0))).astype(f32)
    cst[:, 0:64, 1152] = np.concatenate([inv, inv])
    cst[:, 0:64, 1153] = np.concatenate([np.full(32, math.pi), np.full(32, 2 * math.pi)])
    cst[0, :, 1154] = NEG
    cst[1, :, 1154] = 0.0
    cst[:, :, 1155] = -math.pi
    cst[:, :, 1156] = EPS
    cst[:, :, 1157] = 2.5 * math.pi
    maps = []
    for core in range(P["NCORE"]):
        b, hf = core // 2, core % 2
        maps.append({
            "x_own": np.ascontiguousarray(x[b, hf * NO:(hf + 1) * NO]),
            "x_ctx": np.ascontiguousarray(x[b, 0:NO]),
            "pos_own": np.ascontiguousarray(pos[b, hf * NO:(hf + 1) * NO].reshape(1, NO)),
            "pos_ctx": np.ascontiguousarray(pos[b, 0:NO].reshape(1, NO)),
            "w_in": w_in_r, "w_uq": wuq_r, "w_ukv": wukv_r, "w_o": w_o_r, "w_gate": w_gate_r, "w_up": w_up_r, "w_down": w_down_r,
            "g_rows": g_rows, "g_cols": g_cols, "sinks": sinks, "cst": np.ascontiguousarray(cst[hf]),
        })
    return maps


def run(inp, P, trace=False):
    P = derive(P)
    nc = build(P)
    maps = host_prep(inp, P)
    res = run_bass_kernel_spmd(nc, maps, core_ids=list(range(P["NCORE"])), **({"trace": True} if trace else {}))
    NO = P["NO"]
    outp = np.zeros((P["B"], P["SEQ"], P["D"]), np.float32)
    for core in range(P["NCORE"]):
        b, hf = core // 2, core % 2
        outp[b, hf * NO:(hf + 1) * NO] = res.results[core]["out"]
    return outp, res


def kernel(**inputs):
    outp, _ = run(inputs, FULL)
    return outp
```
